# Optimizing a Trainium2 kernel written in Bass

```python
import jax, jax.numpy as jnp
from jax import lax
import numpy as np

D_MODEL = 1024
BATCH = 4
SEQ = 4096
DEPTH = 4

HEAD_DIM = 64
N_ATT_HEADS = 8
D_ATT = N_ATT_HEADS * HEAD_DIM
N_CONV_GROUPS = 4
D_CONV = N_CONV_GROUPS * HEAD_DIM
N_SGU_GROUPS = 4
D_SGU = N_SGU_GROUPS * HEAD_DIM
SGU_CHUNK = 128
Q_BLOCK = 128
CONV_WIDTH = 3
D_FF = 2816
N_BRANCHES = 3
RMS_EPS = 1e-6
LN_EPS = 1e-5
IN_WIDTHS = (D_ATT, D_ATT, D_ATT, N_ATT_HEADS, D_CONV, D_CONV, D_CONV, D_SGU, D_SGU, N_BRANCHES * D_MODEL)
IN_WIDTH = 3 * D_ATT + N_ATT_HEADS + 3 * D_CONV + 2 * D_SGU + N_BRANCHES * D_MODEL

kernel_name = "fox_shortconv_sgu_gated_hybrid"


def rms_norm(x, g):
    xf = x.astype(jnp.float32)
    y = xf * lax.rsqrt(jnp.mean(xf * xf, axis=-1, keepdims=True) + RMS_EPS)
    return (y * g.astype(jnp.float32)).astype(x.dtype)


def layer_norm(x, g, b):
    xf = x.astype(jnp.float32)
    mu = jnp.mean(xf, axis=-1, keepdims=True)
    xc = xf - mu
    var = jnp.mean(xc * xc, axis=-1, keepdims=True)
    y = xc * lax.rsqrt(var + LN_EPS) * g.astype(jnp.float32) + b.astype(jnp.float32)
    return y.astype(x.dtype)


def split_cols(h, widths):
    offs = np.cumsum(np.array(widths))[:-1].tolist()
    return jnp.split(h, offs, axis=-1)


def causal_dwconv(x, w):
    K = w.shape[0]
    S = x.shape[1]
    xp = jnp.pad(x, ((0, 0), (K - 1, 0), (0, 0)))
    y = xp[:, K - 1:K - 1 + S] * w[K - 1]
    for k in range(K - 1):
        y = y + xp[:, k:k + S] * w[k]
    return y


def fox_attention(q, k, v, logf):
    S = q.shape[1]
    scale = HEAD_DIM ** -0.5
    c = jnp.cumsum(logf, axis=1).transpose(0, 2, 1)
    outs = []
    for i in range(S // Q_BLOCK):
        q0 = i * Q_BLOCK
        q1 = q0 + Q_BLOCK
        s = jnp.einsum('bqhd,bkhd->bhqk', q[:, q0:q1], k[:, :q1]).astype(jnp.float32) * scale
        bias = c[:, :, q0:q1, None] - c[:, :, None, :q1]
        mask = jnp.arange(q0, q1)[:, None] >= jnp.arange(q1)[None, :]
        s = jnp.where(mask, s + bias, -jnp.inf)
        p = jax.nn.softmax(s, axis=-1).astype(v.dtype)
        outs.append(jnp.einsum('bhqk,bkhd->bqhd', p, v[:, :q1]))
    return jnp.concatenate(outs, axis=1)


def short_conv_mixer(b_gate, c_gate, h, conv_w):
    return b_gate * causal_dwconv(c_gate * h, conv_w)


def chunked_sgu(u, v, ln_g, ln_b, w_s, b_s):
    B_, S, _ = u.shape
    u = jax.nn.gelu(u, approximate=True)
    v = layer_norm(jax.nn.gelu(v, approximate=True), ln_g, ln_b)
    n = S // SGU_CHUNK
    vc = v.reshape(B_, n, SGU_CHUNK, N_SGU_GROUPS, HEAD_DIM)
    mask = jnp.tril(jnp.ones((SGU_CHUNK, SGU_CHUNK), w_s.dtype))
    mixed = jnp.einsum('gts,bnsgd->bntgd', w_s * mask, vc) + b_s.T[:, :, None]
    return u * mixed.reshape(B_, S, D_SGU)


def conv_gated_ffn(x, w_up, conv_w, w_down):
    h = causal_dwconv(x @ w_up, conv_w)
    a, b = jnp.split(h, 2, axis=-1)
    return (jax.nn.gelu(a, approximate=True) * b) @ w_down


def setup_inputs(seed: int = 0) -> dict:
    key = jax.random.key(seed)
    ks = jax.random.split(key, 24)
    f32 = jnp.float32

    def nrm(k, shape, scale):
        return jax.random.normal(k, shape, f32) * scale

    L = DEPTH
    return {
        "x": nrm(ks[0], (BATCH, SEQ, D_MODEL), 1.0),
        "pre_mix_g": 1.0 + nrm(ks[1], (L, D_MODEL), 0.02),
        "post_mix_g": 1.0 + nrm(ks[2], (L, D_MODEL), 0.02),
        "pre_ffn_g": 1.0 + nrm(ks[3], (L, D_MODEL), 0.02),
        "post_ffn_g": 1.0 + nrm(ks[4], (L, D_MODEL), 0.02),
        "w_in": nrm(ks[5], (L, D_MODEL, IN_WIDTH), D_MODEL ** -0.5),
        "b_forget": jnp.linspace(1.0, 6.0, N_ATT_HEADS, dtype=f32)[None, :] + nrm(ks[6], (L, N_ATT_HEADS), 0.1),
        "b_gate": nrm(ks[7], (L, N_BRANCHES, D_MODEL), 0.02),
        "conv_mix_w": nrm(ks[8], (L, CONV_WIDTH, D_CONV), CONV_WIDTH ** -0.5),
        "sgu_ln_g": 1.0 + nrm(ks[9], (L, D_SGU), 0.02),
        "sgu_ln_b": nrm(ks[10], (L, D_SGU), 0.02),
        "sgu_w": nrm(ks[11], (L, N_SGU_GROUPS, SGU_CHUNK, SGU_CHUNK), SGU_CHUNK ** -0.5),
        "sgu_b": 1.0 + nrm(ks[12], (L, N_SGU_GROUPS, SGU_CHUNK), 0.02),
        "w_branch_att": nrm(ks[13], (L, D_ATT, D_MODEL), D_ATT ** -0.5),
        "w_branch_conv": nrm(ks[14], (L, D_CONV, D_MODEL), D_CONV ** -0.5),
        "w_branch_sgu": nrm(ks[15], (L, D_SGU, D_MODEL), D_SGU ** -0.5),
        "w_out": nrm(ks[16], (L, D_MODEL, D_MODEL), D_MODEL ** -0.5),
        "w_ffn_up": nrm(ks[17], (L, D_MODEL, 2 * D_FF), D_MODEL ** -0.5),
        "conv_ffn_w": nrm(ks[18], (L, CONV_WIDTH, 2 * D_FF), CONV_WIDTH ** -0.5),
        "w_ffn_down": nrm(ks[19], (L, D_FF, D_MODEL), D_FF ** -0.5),
    }


def reference(x, pre_mix_g, post_mix_g, pre_ffn_g, post_ffn_g, w_in, b_forget, b_gate,
              conv_mix_w, sgu_ln_g, sgu_ln_b, sgu_w, sgu_b, w_branch_att, w_branch_conv,
              w_branch_sgu, w_out, w_ffn_up, conv_ffn_w, w_ffn_down):
    B_, S, D = x.shape
    for l in range(DEPTH):
        xn = rms_norm(x, pre_mix_g[l])
        h = xn @ w_in[l]
        q, k, v, f_logit, bg, cg, hc, u, vs, g_logit = split_cols(h, IN_WIDTHS)
        q = q.reshape(B_, S, N_ATT_HEADS, HEAD_DIM)
        k = k.reshape(B_, S, N_ATT_HEADS, HEAD_DIM)
        v = v.reshape(B_, S, N_ATT_HEADS, HEAD_DIM)
        logf = jax.nn.log_sigmoid((f_logit + b_forget[l]).astype(jnp.float32))
        y_att = fox_attention(q, k, v, logf).reshape(B_, S, D_ATT) @ w_branch_att[l]
        y_conv = short_conv_mixer(bg, cg, hc, conv_mix_w[l]) @ w_branch_conv[l]
        y_sgu = chunked_sgu(u, vs, sgu_ln_g[l], sgu_ln_b[l], sgu_w[l], sgu_b[l]) @ w_branch_sgu[l]
        gates = jax.nn.sigmoid(g_logit.reshape(B_, S, N_BRANCHES, D) + b_gate[l])
        merged = gates[:, :, 0] * y_att + gates[:, :, 1] * y_conv + gates[:, :, 2] * y_sgu
        x = x + rms_norm(merged @ w_out[l], post_mix_g[l])
        xn = rms_norm(x, pre_ffn_g[l])
        x = x + rms_norm(conv_gated_ffn(xn, w_ffn_up[l], conv_ffn_w[l], w_ffn_down[l]), post_ffn_g[l])
    return x
```

```python
import numpy as np
from contextlib import ExitStack
import concourse.bass as bass
import concourse.mybir as mybir
from concourse.bass_utils import run_bass_kernel_spmd

F32 = mybir.dt.float32
BF = mybir.dt.bfloat16
AF = mybir.ActivationFunctionType
ALU = mybir.AluOpType

L = 4
D = 1024
S = 4096
CH = 512
NCH = S // CH
DFF = 2816
NJ = DFF // 128
INW = 5896
NB = S // 128
EPS = 1e-6
LN_EPS = 1e-5
SLOT = 516
NSLOT = 20
NRING = 4

C_Q, C_K, C_V, C_F, C_BG, C_CG, C_HC, C_U, C_VS, C_G = 0, 512, 1024, 1536, 1544, 1800, 2056, 2312, 2568, 2824

PP_GPRE, PP_GPOST, PP_GPREF, PP_GPOSTF = 0, 32, 64, 96
PP_BG = 128
PP_CMW = 224
PP_LNG = 248
PP_LNB = 256
PP_CFW = 264
PP_BF = 792
PP_N = 796

CC_ID = 0
CC_TRI = 128
CC_NEG = 256
CC_N = 384


class Buf:
    __slots__ = ("w", "r")

    def __init__(self):
        self.w = None
        self.r = {}


class Q:
    def __init__(self, P, stream, inc, safe, name):
        self.P, self.stream, self.inc, self.safe, self.name = P, stream, inc, safe, name
        self.sems = [P.new_sem(name)]
        self.ep = 0
        self.cnt = 0
        self.serial = None if safe else Buf()


class Prog:
    LIMIT = 30000

    def __init__(self, nc, es):
        self.nc, self.es = nc, es
        self.ops = {s: [] for s in ("pe", "act", "dve", "pool", "sp")}
        self.waited = {s: {} for s in self.ops}
        self.nsem = 0

    def new_sem(self, name):
        self.nsem += 1
        return self.es.enter_context(self.nc.semaphore(f"{name}_{self.nsem}"))

    def emit(self, q, fns, R=(), W=()):
        if not isinstance(fns, (list, tuple)):
            fns = [fns]
        W = list(W)
        if q.serial is not None:
            W.append(q.serial)
        deps = {}

        def add(m):
            if m is None:
                return
            k = (m[0], m[1])
            if deps.get(k, 0) < m[2]:
                deps[k] = m[2]

        for b in R:
            add(b.w)
        for b in W:
            add(b.w)
            for m in b.r.values():
                add(m)
        waits = []
        wd = self.waited[q.stream]
        for (dq, ep), n in deps.items():
            if dq is q and q.safe:
                continue
            if wd.get((dq, ep), 0) >= n:
                continue
            wd[(dq, ep)] = n
            waits.append((dq.sems[ep], n))
        if q.cnt + q.inc > self.LIMIT:
            q.ep += 1
            q.cnt = 0
            q.sems.append(self.new_sem(q.name))
        q.cnt += q.inc
        mark = (q, q.ep, q.cnt)
        self.ops[q.stream].append((waits, fns, q.sems[q.ep], q.inc))
        for b in R:
            b.r[q] = mark
        for b in W:
            b.w = mark
            b.r = {}
        return mark


def build(depth=L, nch=NCH):
    nc = bass.Bass("TRN2", target_bir_lowering=False)
    x_d = nc.dram_tensor("x", [S, D], F32, kind="ExternalInput").ap()
    y_d = nc.dram_tensor("y", [S, D], F32, kind="ExternalOutput").ap()
    w_in = nc.dram_tensor("w_in", [L, D, INW], F32, kind="ExternalInput").ap()
    w_att = nc.dram_tensor("w_att", [L, 512, D], F32, kind="ExternalInput").ap()
    w_cv = nc.dram_tensor("w_cv", [L, 256, D], F32, kind="ExternalInput").ap()
    w_sg = nc.dram_tensor("w_sg", [L, 256, D], F32, kind="ExternalInput").ap()
    w_out = nc.dram_tensor("w_out", [L, D, D], F32, kind="ExternalInput").ap()
    w_up = nc.dram_tensor("w_up", [L, D, 2 * DFF], F32, kind="ExternalInput").ap()
    w_dn = nc.dram_tensor("w_dn", [L, DFF, D], F32, kind="ExternalInput").ap()
    pp_d = nc.dram_tensor("pp", [128, PP_N], F32, kind="ExternalInput").ap()
    cc_d = nc.dram_tensor("cc", [128, CC_N], F32, kind="ExternalInput").ap()
    swT_d = nc.dram_tensor("swT", [128, L * 4 * 128], F32, kind="ExternalInput").ap()
    sgb_d = nc.dram_tensor("sgb", [128, L * 2 * 128], F32, kind="ExternalInput").ap()
    sel_d = nc.dram_tensor("sel", [8, 1024], F32, kind="ExternalInput").ap()
    xT_d = nc.dram_tensor("xT", [8, 128, S], F32, kind="Internal").ap()
    NT = 35
    wbf_d = nc.dram_tensor("wbf", [L, NT, 128, 4096], BF, kind="Internal").ap()

    es = ExitStack()
    with es:
        def sb(name, shape, dt):
            return es.enter_context(nc.sbuf_tensor("s_" + name, shape, dt))

        xs = sb("xs", [128, 8, CH], F32)
        xn = sb("xn", [128, 8, CH], BF)
        zt = sb("zt", [128, 8 * CH], F32)
        KT = sb("KT", [128, 4, S], BF)
        VA = sb("VA", [128, NB, 8, 65], BF)
        negc = sb("negc", [128, NB, 8], F32)
        arena = sb("arena", [128, NSLOT * SLOT], F32)
        st = [sb(f"st{i}", [128, CH], F32) for i in range(3)]
        pT = [sb(f"pT{i}", [128, CH], BF) for i in range(3)]
        cT = sb("cT", [8, CH], F32)
        mT = sb("mT", [128, CH], BF)
        carry = sb("carry", [8, 1], F32)
        ones8 = sb("ones8", [8, CH], BF)
        rinv = sb("rinv", [128, 4], F32)
        wr = [sb(f"wr{i}", [128, 8, 512], BF) for i in range(NRING)]
        pp = sb("pp", [128, PP_N], F32)
        hbg = sb("hbg", [128, 96], F32)
        nbf = sb("nbf", [8, 4], F32)
        cc = sb("cc", [128, CC_N], F32)
        identb = sb("identb", [128, 128], BF)
        negmb = sb("negmb", [128, 128], BF)
        selb = sb("selb", [128, 1024], BF)
        onesb = sb("onesb", [128, 128], BF)
        WmT = sb("WmT", [128, L * 4 * 128], BF)
        sgb = sb("sgbs", [128, L * 2 * 128], F32)
        cvh = sb("cvh", [128, 2, 2], F32)
        fha = sb("fha", [128, 2 * NJ, 2], F32)
        ps = [es.enter_context(nc.psum_tensor(f"ps{i}", [128, 512], F32)) for i in range(8)]

        P = Prog(nc, es)
        PE = Q(P, "pe", 1, True, "pe")
        ACT = Q(P, "act", 1, False, "act")
        DVE = Q(P, "dve", 1, False, "dve")
        POOL = Q(P, "pool", 1, False, "pool")
        RQ = [Q(P, "sp", 16, False, f"rq{i}") for i in range(NRING)]
        CVQ = [Q(P, "pool", 16, False, f"cvq{i}") for i in range(4)]
        XLD = Q(P, "pool", 16, False, "xld")
        XST = Q(P, "pool", 16, False, "xst")
        CQ = Q(P, "sp", 16, False, "cq")
        emit = P.emit

        xsb, xnb, ztb = Buf(), Buf(), Buf()
        KTb = [Buf() for _ in range(NCH)]
        VAb = [Buf() for _ in range(NCH)]
        ngb = [Buf() for _ in range(NCH)]
        slot = [Buf() for _ in range(NSLOT)]
        stb = [Buf() for _ in range(4)]
        pTb = [Buf(), Buf(), Buf()]
        cTb, mTb, carb, rinvb = Buf(), Buf(), Buf(), Buf()
        spT = st[1][0:8, :]
        spTb = stb[1]
        wrb = [Buf() for _ in range(NRING)]
        cst = Buf()
        cvhb, fhab = Buf(), [Buf() for _ in range(2 * NJ)]
        psb = [Buf() for _ in range(8)]
        xTb = [Buf() for _ in range(NCH)]
        yb = Buf()

        def av(s0, n, dt, dims):
            ap = arena[:, s0 * SLOT:(s0 + n) * SLOT]
            if dt is BF:
                ap = ap.bitcast(BF)
            tot = 1
            for d_ in dims:
                tot *= d_
            ap = ap[:, 0:tot]
            if len(dims) == 2:
                ap = ap.rearrange("p (a b) -> p a b", a=dims[0])
            return ap

        QT = av(16, 4, BF, [8, CH]); QTb = slot[16:20]
        ycv = av(2, 1, BF, [2, CH]); ycvb = slot[2:3]
        ysg = av(3, 1, BF, [2, CH]); ysgb = slot[3:4]
        gu = av(4, 1, BF, [2, CH]); gub = slot[4:5]
        pbuf = [av(5, 1, F32, [SLOT]), av(6, 1, F32, [SLOT])]; pbufb = [slot[5:6], slot[6:7]]
        tmph = av(7, 1, F32, [CH]); tmphb = slot[7:8]
        cacc = av(8, 1, F32, [CH]); caccb = slot[8:9]
        gv = av(9, 2, F32, [2, CH]); gvb_ = slot[9:11]
        vn = av(11, 2, F32, [2, CH]); vnb = slot[11:13]
        gvh = av(13, 1, BF, [2, CH]); gvhb = slot[13:14]
        gv2 = av(14, 1, BF, [2, CH]); gv2b = slot[14:15]
        vtok = av(15, 1, BF, [4, 256]); vtokb = slot[15:16]
        atok = av(5, 4, F32, [4, 512]); atokb = slot[5:9]
        attT = av(9, 2, BF, [4, CH]); attTb = slot[9:11]
        th = av(11, 1, F32, [CH]); thb = slot[11:12]
        macc = av(12, 1, F32, [CH]); maccb = slot[12:13]
        mtmp = av(14, 1, F32, [CH]); mtmpb = slot[14:15]
        mrg = av(5, 4, BF, [8, CH]); mrgb = slot[5:9]
        gff = av(0, 11, BF, [NJ, CH])
        def gffb(j):
            return slot[(j * CH) // (2 * SLOT):((j + 1) * CH - 1) // (2 * SLOT) + 1]
        hP = [av(11 + i, 1, F32, [SLOT]) for i in range(3)]; hPb = [slot[11 + i:12 + i] for i in range(3)]
        cP = [av(14 + i, 1, F32, [CH]) for i in range(3)]; cPb = [slot[14 + i:15 + i] for i in range(3)]
        gP = [av(17 + i, 1, F32, [CH]) for i in range(2)]; gPb = [slot[17 + i:18 + i] for i in range(2)]
        hPh = [Buf() for _ in range(3)]
        ffn_rr = [0, 0]
        thP = [av(i, 1, F32, [CH]) for i in (11, 13, 15)]; thPb = [slot[i:i + 1] for i in (11, 13, 15)]
        mtP = [av(i, 1, F32, [CH]) for i in (14, 0)]; mtPb = [slot[i:i + 1] for i in (14, 0)]
        macc4 = [av(i, 1, F32, [CH]) for i in (12, 16, 17, 18)]; macc4b = [slot[i:i + 1] for i in (12, 16, 17, 18)]
        mrg_rr = [0, 0]

        z3 = zt[:, :].rearrange("p (a b) -> p a b", a=8)
        xtok = zt[:, :].rearrange("p (a b) -> p a b", a=4)

        ps_rr = [0]

        def ps_next(exclude=()):
            while True:
                i = ps_rr[0]
                ps_rr[0] = (i + 1) % 8
                if i not in exclude:
                    return i

        ring_rr = [0]

        wl_cnt = [0]
        cur_l = [0]
        wcvb = {}
        cv_rr = [0]

        def wbf3(l_, tid):
            return wbf_d[l_, tid].rearrange("p (k c) -> p k c", k=8)

        def tile_specs(l_):
            Rr = lambda ap: ap.rearrange("(k p) c -> p k c", p=128)
            sp_ = []
            for c0, w in ((C_Q, 512), (C_K, 512), (C_V, 512), (C_F, 8), (C_BG, 512), (C_HC, 512), (C_VS, 256)):
                sp_.append([(Rr(w_in[l_, :, c0:c0 + w]), 0, 8, w)])
            for dh in range(2):
                sp_.append([(Rr(w_att[l_, :, dh * 512:dh * 512 + 512]), 0, 4, 512),
                            (Rr(w_cv[l_, :, dh * 512:dh * 512 + 512]), 4, 2, 512),
                            (Rr(w_sg[l_, :, dh * 512:dh * 512 + 512]), 6, 2, 512)])
                for i in range(3):
                    sp_.append([(Rr(w_in[l_, :, C_G + i * D + dh * 512:C_G + i * D + dh * 512 + 512]), 0, 8, 512)])
            for eh in range(2):
                sp_.append([(Rr(w_out[l_, :, eh * 512:eh * 512 + 512]), 0, 8, 512)])
            for tt in range(6):
                w_ = 512 if tt < 5 else 256
                sp_.append([(Rr(w_up[l_, :, tt * 512:tt * 512 + w_]), 0, 8, w_)])
                sp_.append([(Rr(w_up[l_, :, DFF + tt * 512:DFF + tt * 512 + w_]), 0, 8, w_)])
            for eh in range(2):
                for kg in range(3):
                    nk = 8 if kg < 2 else 6
                    sp_.append([(Rr(w_dn[l_, kg * 1024:kg * 1024 + nk * 128, eh * 512:eh * 512 + 512]), 0, nk, 512)])
            assert len(sp_) == NT
            return sp_

        def conv_tile(l_, tid, parts):
            b_ = wcvb.setdefault((l_, tid), Buf())
            for (src, k0, nk, w) in parts:
                q_ = CVQ[cv_rr[0] % 4]
                cv_rr[0] += 1
                emit(q_, lambda e, src=src, k0=k0, nk=nk, w=w: e.dma_start(out=wbf3(l_, tid)[:, k0:k0 + nk, 0:w], in_=src), W=[b_])

        def wload(src_ap, nk, w):
            tid = wl_cnt[0]
            wl_cnt[0] += 1
            l_ = cur_l[0]
            i = ring_rr[0]
            ring_rr[0] = (i + 1) % NRING
            emit(RQ[i], lambda e: e.dma_start(out=wr[i][:, 0:nk, 0:w], in_=wbf3(l_, tid)[:, 0:nk, 0:w]), R=[wcvb[(l_, tid)]], W=[wrb[i]])
            return i

        def mm(out_ap, pairs, R, W, extra=()):
            n = len(pairs)
            fns = [(lambda e, a=a, b=b, i=i: e.matmul(out_ap, a, b, start=(i == 0), stop=(i == n - 1 and not extra)))
                   for i, (a, b) in enumerate(pairs)]
            fns += list(extra)
            emit(PE, fns, R, W)

        emit(CQ, lambda e: e.dma_start(out=pp[:, :], in_=pp_d[:, :]), W=[cst])
        emit(CQ, lambda e: e.dma_start(out=cc[:, :], in_=cc_d[:, :]), W=[cst])
        emit(CQ, lambda e: e.dma_start(out=zt[:, 0:L * 512], in_=swT_d[:, :]), W=[cst, ztb])
        emit(CQ, lambda e: e.dma_start(out=sgb[:, :], in_=sgb_d[:, :]), W=[cst])
        emit(DVE, lambda e: e.tensor_copy(identb[:, :], cc[:, CC_ID:CC_ID + 128]), R=[cst], W=[cst])
        emit(DVE, lambda e: e.tensor_copy(negmb[:, :], cc[:, CC_NEG:CC_NEG + 128]), R=[cst], W=[cst])
        emit(CQ, lambda e: e.dma_start(out=zt[0:8, 2048:3072], in_=sel_d[:, :]), W=[cst, ztb])
        emit(DVE, lambda e: e.memset(selb[:, :], 0.0), W=[cst])
        emit(DVE, lambda e: e.memset(mT[:, :], 0.0), W=[mTb])
        emit(DVE, lambda e: e.tensor_copy(selb[0:8, :], zt[0:8, 2048:3072]), R=[cst, ztb], W=[cst])
        emit(DVE, lambda e: e.memset(onesb[:, :], 1.0), W=[cst])
        emit(DVE, lambda e: e.memset(ones8[:, :], 1.0), W=[cst])
        emit(DVE, lambda e: e.memset(VA[:, :, :, 64:65], 1.0), W=VAb)
        emit(DVE, lambda e: e.tensor_scalar(hbg[:, :], pp[:, PP_BG:PP_BG + 96], 0.5, None, ALU.mult), R=[cst], W=[cst])
        emit(DVE, lambda e: e.tensor_scalar(nbf[:, :], pp[0:8, PP_BF:PP_BF + 4], -1.0, None, ALU.mult), R=[cst], W=[cst])
        for l in range(depth):
            for g in range(4):
                o = (l * 4 + g) * 128
                emit(DVE, lambda e, o=o: e.tensor_tensor(WmT[:, o:o + 128], zt[:, o:o + 128],
                                                       cc[:, CC_TRI:CC_TRI + 128], ALU.mult), R=[cst, ztb], W=[cst])
        ident = cc[:, CC_ID:CC_ID + 128]

        def rms_stats(src3, srcb, eps, mean_scale, have_sq=False):
            if not have_sq:
                emit(ACT, lambda e: e.activation(out=xn[:, :, :], in_=src3, func=AF.Square), R=srcb, W=[xnb])
            b = ps_next()
            mm(ps[b][:, :], [(onesb[:, :], xn[:, kc, :]) for kc in range(8)], R=[xnb, cst], W=[psb[b]])
            emit(ACT, lambda e: e.activation(out=st[0][:, :], in_=ps[b][:, :], func=AF.Ln, bias=eps, scale=mean_scale),
                 R=[psb[b]], W=[stb[0]])
            emit(ACT, lambda e: e.activation(out=st[0][:, :], in_=st[0][:, :], func=AF.Exp, scale=-0.5),
                 R=[stb[0]], W=[stb[0]])

        def prenorm(goff, have_sq=False):
            rms_stats(xs[:, :, :], [xsb], EPS, 1.0 / D, have_sq)
            for kc in range(8):
                emit(DVE, lambda e, kc=kc: e.scalar_tensor_tensor(out=xn[:, kc, :], in0=xs[:, kc, :],
                                                                 scalar=pp[:, goff + kc:goff + kc + 1], in1=st[0][:, :],
                                                                 op0=ALU.mult, op1=ALU.mult),
                     R=[xsb, stb[0], cst], W=[xnb])

        def postnorm_residual(goff, eps, sq_after=False):
            rms_stats(z3, [ztb], eps, 1.0 / D, True)
            for kc in range(8):
                emit(DVE, lambda e, kc=kc: e.scalar_tensor_tensor(out=z3[:, kc, :], in0=z3[:, kc, :],
                                                                 scalar=pp[:, goff + kc:goff + kc + 1], in1=st[0][:, :],
                                                                 op0=ALU.mult, op1=ALU.mult),
                     R=[stb[0], cst], W=[ztb])
            for kc in range(8):
                emit(DVE, lambda e, kc=kc: e.tensor_tensor(xs[:, kc, :], xs[:, kc, :], z3[:, kc, :], ALU.add), R=[ztb], W=[xsb])
                if sq_after:
                    emit(ACT, lambda e, kc=kc: e.activation(out=xn[:, kc, :], in_=xs[:, kc, :], func=AF.Square), R=[xsb], W=[xnb])

        specs = [tile_specs(l_) for l_ in range(depth)]
        for tid in range(NT):
            conv_tile(0, tid, specs[0][tid])
        for l in range(depth):
            for c in range(nch):
                t0 = c * CH
                wl_cnt[0] = 0
                cur_l[0] = l
                if l == 0:
                    emit(XLD, lambda e, t0=t0: e.dma_start(
                        out=xtok, in_=x_d[t0:t0 + CH, :].rearrange("(tb p) d -> p tb d", p=128)), W=[ztb])
                    for kc in range(8):
                        b = ps_next()
                        emit(PE, [(lambda e, tb=tb, kc=kc, b=b: e.transpose(ps[b][:, tb * 128:(tb + 1) * 128],
                                                                          xtok[:, tb, kc * 128:(kc + 1) * 128], ident))
                                  for tb in range(4)], R=[ztb, cst], W=[psb[b]])
                        emit(ACT if kc % 2 else DVE,
                             (lambda e, kc=kc, b=b: e.activation(out=xs[:, kc, :], in_=ps[b][:, :], func=AF.Copy)) if kc % 2
                             else (lambda e, kc=kc, b=b: e.tensor_copy(xs[:, kc, :], ps[b][:, :])),
                             R=[psb[b]], W=[xsb])
                else:
                    emit(XLD, lambda e, t0=t0: e.dma_start(
                        out=xs[:, :, :], in_=xT_d[:, :, t0:t0 + CH].rearrange("k p t -> p k t")), R=[xTb[c]], W=[xsb])

                prenorm(PP_GPRE + l * 8)

                wi = wload(w_in[l, :, C_Q:C_Q + 512].rearrange("(k p) c -> p k c", p=128), 8, 512)
                for fc in range(4):
                    b = ps_next()
                    mm(ps[b][:, :], [(wr[wi][:, kc, fc * 128:(fc + 1) * 128], xn[:, kc, :]) for kc in range(8)],
                       R=[wrb[wi], xnb], W=[psb[b]])
                    if fc == 0:
                        emit(DVE, lambda e: e.memset(QT, 0.0), W=QTb)
                    for hh in range(2):
                        emit(ACT, lambda e, fc=fc, b=b, hh=hh: e.activation(out=QT[hh * 64:hh * 64 + 64, 2 * fc + hh, :],
                                                                          in_=ps[b][hh * 64:hh * 64 + 64, :], func=AF.Copy, scale=0.125),
                             R=[psb[b]], W=QTb)
                wi = wload(w_in[l, :, C_K:C_K + 512].rearrange("(k p) c -> p k c", p=128), 8, 512)
                for fc in range(4):
                    b = ps_next()
                    mm(ps[b][:, :], [(wr[wi][:, kc, fc * 128:(fc + 1) * 128], xn[:, kc, :]) for kc in range(8)],
                       R=[wrb[wi], xnb], W=[psb[b]])
                    emit(DVE, lambda e, fc=fc, b=b, t0=t0: e.tensor_copy(KT[:, fc, t0:t0 + CH], ps[b][:, :]),
                         R=[psb[b]], W=[KTb[c]])
                wi = wload(w_in[l, :, C_V:C_V + 512].rearrange("(k p) c -> p k c", p=128), 8, 512)
                for tb in range(4):
                    b = ps_next()
                    mm(ps[b][:, :], [(xn[:, kc, tb * 128:(tb + 1) * 128], wr[wi][:, kc, 0:512]) for kc in range(8)],
                       R=[wrb[wi], xnb], W=[psb[b]])
                    emit(ACT if tb % 2 else DVE,
                         (lambda e, tb=tb, b=b, c=c: e.activation(out=VA[:, c * 4 + tb, :, 0:64],
                                                              in_=ps[b][:, :].rearrange("p (h d) -> p h d", h=8), func=AF.Copy))
                         if tb % 2 else
                         (lambda e, tb=tb, b=b, c=c: e.tensor_copy(VA[:, c * 4 + tb, :, 0:64],
                                                               ps[b][:, :].rearrange("p (h d) -> p h d", h=8))),
                         R=[psb[b]], W=[VAb[c]])
                wi = wload(w_in[l, :, C_F:C_F + 8].rearrange("(k p) c -> p k c", p=128), 8, 8)
                b = ps_next()
                mm(ps[b][0:8, :], [(wr[wi][:, kc, 0:8], xn[:, kc, :]) for kc in range(8)], R=[wrb[wi], xnb], W=[psb[b]])
                emit(ACT, lambda e, b=b, l=l: e.activation(out=spT[:, :], in_=ps[b][0:8, :], func=AF.Exp,
                                                       bias=nbf[:, l:l + 1], scale=-1.0), R=[psb[b], cst], W=[spTb])
                emit(ACT, lambda e: e.activation(out=spT[:, :], in_=spT[:, :], func=AF.Ln, bias=1.0, scale=1.0),
                     R=[spTb], W=[spTb])
                if c == 0:
                    emit(DVE, lambda e: e.memset(carry[:, :], 0.0), W=[carb])
                emit(DVE, lambda e: e.tensor_tensor_scan(cT[:, :], ones8[:, :], spT[:, :], carry[:, 0:1], ALU.mult, ALU.subtract),
                     R=[spTb, carb, cst], W=[cTb])
                emit(DVE, lambda e: e.tensor_copy(carry[:, 0:1], cT[:, CH - 1:CH]), R=[cTb], W=[carb])
                emit(DVE, lambda e: e.tensor_copy(mT[0:8, :], cT[:, :]), R=[cTb], W=[mTb])
                b = ps_next()
                emit(PE, [(lambda e, tb=tb, b=b: e.matmul(ps[b][:, tb * 8:(tb + 1) * 8], cT[:, tb * 128:(tb + 1) * 128],
                                                        cc[0:8, CC_ID:CC_ID + 8], start=True, stop=True)) for tb in range(4)],
                     R=[cTb, cst], W=[psb[b]])
                emit(ACT, lambda e, b=b, c=c: e.activation(out=negc[:, c * 4:(c + 1) * 4, :],
                                                       in_=ps[b][:, 0:32].rearrange("p (a h) -> p a h", a=4),
                                                       func=AF.Copy, scale=-1.0), R=[psb[b]], W=[ngb[c]])
                wa = wload(w_in[l, :, C_BG:C_BG + 512].rearrange("(k p) c -> p k c", p=128), 8, 512)
                wb = wload(w_in[l, :, C_HC:C_HC + 512].rearrange("(k p) c -> p k c", p=128), 8, 512)
                wv = wload(w_in[l, :, C_VS:C_VS + 256].rearrange("(k p) c -> p k c", p=128), 8, 256)
                for j in range(2):
                    b = ps_next()
                    mm(ps[b][:, :], [(wr[wb][:, kc, 256 + j * 128:256 + (j + 1) * 128], xn[:, kc, :]) for kc in range(8)],
                       R=[wrb[wb], xnb], W=[psb[b]])
                    emit(ACT, lambda e, j=j, b=b: e.activation(out=gu[:, j, :], in_=ps[b][:, :], func=AF.Gelu_apprx_tanh),
                         R=[psb[b]], W=gub)
                    b = ps_next()
                    mm(ps[b][:, :], [(wr[wv][:, kc, j * 128:(j + 1) * 128], xn[:, kc, :]) for kc in range(8)],
                       R=[wrb[wv], xnb], W=[psb[b]])
                    emit(ACT, lambda e, j=j, b=b: e.activation(out=gv[:, j, :], in_=ps[b][:, :], func=AF.Gelu_apprx_tanh),
                         R=[psb[b]], W=gvb_)
                emit(DVE, lambda e: e.tensor_copy(gvh, gv), R=gvb_, W=gvhb)
                emit(ACT, lambda e: e.activation(out=gv2, in_=gv, func=AF.Square), R=gvb_, W=gv2b)
                b1 = ps_next()
                mm(ps[b1][:, :], [(onesb[:, :], gvh[:, j, :]) for j in range(2)], R=gvhb + [cst], W=[psb[b1]])
                b2 = ps_next()
                mm(ps[b2][:, :], [(onesb[:, :], gv2[:, j, :]) for j in range(2)], R=gv2b + [cst], W=[psb[b2]])
                emit(ACT, lambda e, b1=b1: e.activation(out=st[1][:, :], in_=ps[b1][:, :], func=AF.Copy, scale=1.0 / 256),
                     R=[psb[b1]], W=[stb[1]])
                emit(DVE, lambda e: e.tensor_tensor(st[2][:, :], st[1][:, :], st[1][:, :], ALU.mult), R=[stb[1]], W=[stb[2]])
                emit(DVE, lambda e, b2=b2: e.scalar_tensor_tensor(out=st[2][:, :], in0=ps[b2][:, :], scalar=1.0 / 256, in1=st[2][:, :],
                                                                op0=ALU.mult, op1=ALU.subtract), R=[psb[b2], stb[2]], W=[stb[2]])
                emit(ACT, lambda e: e.activation(out=st[2][:, :], in_=st[2][:, :], func=AF.Ln, bias=LN_EPS, scale=1.0),
                     R=[stb[2]], W=[stb[2]])
                emit(ACT, lambda e: e.activation(out=st[2][:, :], in_=st[2][:, :], func=AF.Exp, scale=-0.5),
                     R=[stb[2]], W=[stb[2]])
                for j in range(2):
                    emit(DVE, lambda e, j=j: e.tensor_tensor(vn[:, j, :], gv[:, j, :], st[1][:, :], ALU.subtract),
                         R=gvb_ + [stb[1]], W=vnb)
                    emit(DVE, lambda e, j=j: e.tensor_tensor(vn[:, j, :], vn[:, j, :], st[2][:, :], ALU.mult),
                         R=[stb[2]], W=vnb)
                    og, ob = PP_LNG + l * 2 + j, PP_LNB + l * 2 + j
                    emit(DVE, lambda e, j=j, og=og, ob=ob: e.tensor_scalar(vn[:, j, :], vn[:, j, :], pp[:, og:og + 1], pp[:, ob:ob + 1],
                                                                         ALU.mult, ALU.add), R=[cst], W=vnb)
                for j in range(2):
                    bh = ps_next()
                    mm(ps[bh][:, :], [(wr[wb][:, kc, j * 128:(j + 1) * 128], xn[:, kc, :]) for kc in range(8)],
                       R=[wrb[wb], xnb], W=[psb[bh]])
                    emit(ACT, lambda e, bh=bh: e.activation(out=tmph, in_=ps[bh][:, :], func=AF.Copy), R=[psb[bh]], W=tmphb)
                    bc_ = ps_next()
                    mm(ps[bc_][:, :], [(wr[wa][:, kc, 256 + j * 128:256 + (j + 1) * 128], xn[:, kc, :]) for kc in range(8)],
                       R=[wrb[wa], xnb], W=[psb[bc_]])
                    pb = pbuf[j]
                    if c == 0:
                        emit(DVE, lambda e, pb=pb: e.memset(pb[:, 0:2], 0.0), W=pbufb[j])
                    else:
                        emit(DVE, lambda e, pb=pb, j=j: e.tensor_copy(pb[:, 0:2], cvh[:, j, :]), R=[cvhb], W=pbufb[j])
                    emit(DVE, lambda e, pb=pb, bc_=bc_: e.tensor_tensor(pb[:, 2:2 + CH], ps[bc_][:, :], tmph, ALU.mult),
                         R=[psb[bc_]] + tmphb, W=pbufb[j])
                    emit(DVE, lambda e, pb=pb, j=j: e.tensor_copy(cvh[:, j, :], pb[:, CH:CH + 2]), R=pbufb[j], W=[cvhb])
                    k0 = PP_CMW + (l * 2 + j) * 3
                    emit(DVE, lambda e, pb=pb, k0=k0: e.tensor_scalar(cacc, pb[:, 2:2 + CH], pp[:, k0 + 2:k0 + 3], None, ALU.mult),
                         R=pbufb[j] + [cst], W=caccb)
                    emit(DVE, lambda e, pb=pb, k0=k0: e.scalar_tensor_tensor(out=cacc, in0=pb[:, 1:1 + CH], scalar=pp[:, k0 + 1:k0 + 2],
                                                                           in1=cacc, op0=ALU.mult, op1=ALU.add),
                         R=pbufb[j] + [cst], W=caccb)
                    emit(DVE, lambda e, pb=pb, k0=k0: e.scalar_tensor_tensor(out=cacc, in0=pb[:, 0:CH], scalar=pp[:, k0:k0 + 1],
                                                                           in1=cacc, op0=ALU.mult, op1=ALU.add),
                         R=pbufb[j] + [cst], W=caccb)
                    bb_ = ps_next()
                    mm(ps[bb_][:, :], [(wr[wa][:, kc, j * 128:(j + 1) * 128], xn[:, kc, :]) for kc in range(8)],
                       R=[wrb[wa], xnb], W=[psb[bb_]])
                    emit(DVE, lambda e, j=j, bb_=bb_: e.tensor_tensor(ycv[:, j, :], ps[bb_][:, :], cacc, ALU.mult),
                         R=[psb[bb_]] + caccb, W=ycvb)
                if l + 1 < depth:
                    per = -(-NT // nch)
                    for tid in range(c * per, min(NT, (c + 1) * per)):
                        conv_tile(l + 1, tid, specs[l + 1][tid])
                nkb = 4 * c + 4
                tiles = [(h, kb) for h in range(8) for kb in range(nkb)]
                obank = {}
                tinfo = {}

                def s_stage(i):
                    h, kb = tiles[i]
                    fc, r0 = h // 2, (h % 2) * 64
                    if h not in obank:
                        obank[h] = ps_next(exclude=tuple(obank.values()))
                    excl = tuple(obank.values())
                    kc_ = kb // 4
                    r = kb - 4 * c
                    bs = ps_next(exclude=excl)
                    kT = KT[:, fc, kb * 128:(kb + 1) * 128]
                    fns = []
                    if r < 0:
                        fns.append(lambda e: e.matmul(ps[bs][:, :], kT, QT[:, h, :], start=True, stop=False))
                        fns.append(lambda e: e.matmul(ps[bs][:, :], selb[:, h * 128:(h + 1) * 128], mT[:, :], start=False, stop=True))
                        q0 = 0
                    else:
                        q0 = r * 128
                        fns.append(lambda e: e.matmul(ps[bs][:, q0:q0 + 128], kT, QT[:, h, q0:q0 + 128], start=True, stop=False))
                        fns.append(lambda e: e.matmul(ps[bs][:, q0:q0 + 128], selb[:, h * 128:(h + 1) * 128], mT[:, q0:q0 + 128], start=False, stop=False))
                        fns.append(lambda e: e.matmul(ps[bs][:, q0:q0 + 128], identb[:, :], negmb[:, :], start=False, stop=True))
                        if q0 + 128 < CH:
                            fns.append(lambda e: e.matmul(ps[bs][:, q0 + 128:CH], kT, QT[:, h, q0 + 128:CH], start=True, stop=False))
                            fns.append(lambda e: e.matmul(ps[bs][:, q0 + 128:CH], selb[:, h * 128:(h + 1) * 128], mT[:, q0 + 128:CH], start=False, stop=True))
                    emit(PE, fns, R=[KTb[kc_], mTb, cst] + QTb, W=[psb[bs]])
                    pi = i % 3
                    emit(ACT, lambda e: e.activation(out=pT[pi][:, q0:CH], in_=ps[bs][:, q0:CH], func=AF.Exp,
                                                     bias=negc[:, kb, h:h + 1], scale=1.0),
                         R=[psb[bs], ngb[kc_]], W=[pTb[pi]])
                    tinfo[i] = (q0, pi, kc_)

                def pv_stage(i):
                    h, kb = tiles[i]
                    q0, pi, kc_ = tinfo.pop(i)
                    bo = obank[h]
                    pso = ps[bo][:, 0:260].rearrange("p (a b) -> p a b", a=4)
                    fns = []
                    for qb in range(q0 // 128, 4):
                        fns.append(lambda e, qb=qb: e.matmul(
                            pso[:, qb, :], pT[pi][:, qb * 128:(qb + 1) * 128], VA[:, kb, h, :],
                            start=(kb == 0 and qb == 0), stop=(kb == 4 * c + qb), skip_group_check=True))
                    emit(PE, fns, R=[pTb[pi], VAb[kc_]], W=[psb[bo]])
                    if kb == nkb - 1:
                        emit(DVE, lambda e: e.reciprocal(rinv[:, :], pso[:, :, 64]), R=[psb[bo]], W=[rinvb])
                        for qb in range(4):
                            emit(DVE, lambda e, qb=qb: e.tensor_scalar(atok[:, qb, h * 64:(h + 1) * 64], pso[:, qb, 0:64],
                                                                      rinv[:, qb:qb + 1], None, ALU.mult),
                                 R=[psb[bo], rinvb], W=atokb)
                        if h - 1 in obank:
                            del obank[h - 1]

                s_stage(0)
                if len(tiles) > 1:
                    s_stage(1)
                for i in range(len(tiles)):
                    if i + 2 < len(tiles):
                        s_stage(i + 2)
                    pv_stage(i)
                for tb in range(4):
                    b = ps_next()
                    emit(PE, [(lambda e, j=j, tb=tb, b=b: e.transpose(ps[b][:, j * 128:(j + 1) * 128],
                                                                    vn[:, j, tb * 128:(tb + 1) * 128], ident)) for j in range(2)],
                         R=vnb + [cst], W=[psb[b]])
                    emit(ACT, lambda e, tb=tb, b=b: e.activation(out=vtok[:, tb, :], in_=ps[b][:, 0:256], func=AF.Copy),
                         R=[psb[b]], W=vtokb)
                for j in range(2):
                    bA, bB = ps_next(), ps_next()
                    fns = []
                    for tb in range(4):
                        for hh, bx in ((0, bA), (1, bB)):
                            o = (l * 4 + 2 * j + hh) * 128
                            fns.append(lambda e, tb=tb, bx=bx, o=o, j=j: e.matmul(
                                ps[bx][:, tb * 128:(tb + 1) * 128], vtok[:, tb, j * 128:(j + 1) * 128], WmT[:, o:o + 128],
                                start=True, stop=True))
                    emit(PE, fns, R=vtokb + [cst], W=[psb[bA], psb[bB]])
                    ob = (l * 2 + j) * 128
                    for hh, bx in ((0, bA), (1, bB)):
                        r0, r1 = hh * 64, hh * 64 + 64
                        for tb in range(4):
                            emit(DVE, lambda e, bx=bx, r0=r0, r1=r1, ob=ob, tb=tb: e.tensor_tensor(
                                th[r0:r1, tb * 128:(tb + 1) * 128], ps[bx][r0:r1, tb * 128:(tb + 1) * 128],
                                sgb[r0:r1, ob:ob + 128], ALU.add), R=[psb[bx], cst], W=thb)
                    emit(DVE, lambda e, j=j: e.tensor_tensor(ysg[:, j, :], th, gu[:, j, :], ALU.mult), R=thb + gub, W=ysgb)

                for fc in range(4):
                    b = ps_next()
                    emit(PE, [(lambda e, qb=qb, fc=fc, b=b: e.transpose(ps[b][:, qb * 128:(qb + 1) * 128],
                                                                      atok[:, qb, fc * 128:(fc + 1) * 128], ident)) for qb in range(4)],
                         R=atokb + [cst], W=[psb[b]])
                    emit(ACT, lambda e, fc=fc, b=b: e.activation(out=attT[:, fc, :], in_=ps[b][:, :], func=AF.Copy),
                         R=[psb[b]], W=attTb)

                for dh in range(2):
                    ib = wload(None, 8, 512)
                    for i in range(3):
                        wgi = wload(w_in[l, :, C_G + i * D + dh * 512:C_G + i * D + dh * 512 + 512].rearrange("(k p) c -> p k c", p=128), 8, 512)
                        for dd in range(4):
                            d = dh * 4 + dd
                            cs = slice(dd * 128, (dd + 1) * 128)
                            if i == 0:
                                src_, srcb_ = [(wr[ib][:, k, cs], attT[:, k, :]) for k in range(4)], attTb
                            elif i == 1:
                                src_, srcb_ = [(wr[ib][:, 4 + k, cs], ycv[:, k, :]) for k in range(2)], ycvb
                            else:
                                src_, srcb_ = [(wr[ib][:, 6 + k, cs], ysg[:, k, :]) for k in range(2)], ysgb
                            ti = mrg_rr[0] % 3
                            mrg_rr[0] += 1
                            bg_ = ps_next()
                            mm(ps[bg_][:, :], [(wr[wgi][:, kc, cs], xn[:, kc, :]) for kc in range(8)],
                               R=[wrb[wgi], xnb], W=[psb[bg_]])
                            ob = l * 24 + i * 8 + d
                            emit(ACT, lambda e, bg_=bg_, ob=ob, ti=ti: e.activation(out=thP[ti], in_=ps[bg_][:, :], func=AF.Tanh,
                                                                                bias=hbg[:, ob:ob + 1], scale=0.5),
                                 R=[psb[bg_], cst], W=thPb[ti])
                            by = ps_next()
                            mm(ps[by][:, :], src_, R=[wrb[ib]] + srcb_, W=[psb[by]])
                            if i == 0:
                                emit(DVE, lambda e, by=by, ti=ti, dd=dd: e.scalar_tensor_tensor(out=macc4[dd], in0=thP[ti], scalar=1.0, in1=ps[by][:, :],
                                                                                             op0=ALU.add, op1=ALU.mult),
                                     R=thPb[ti] + [psb[by]], W=macc4b[dd])
                            else:
                                mi = mrg_rr[1] % 2
                                mrg_rr[1] += 1
                                emit(DVE, lambda e, by=by, ti=ti, mi=mi: e.scalar_tensor_tensor(out=mtP[mi], in0=thP[ti], scalar=1.0, in1=ps[by][:, :],
                                                                                             op0=ALU.add, op1=ALU.mult),
                                     R=thPb[ti] + [psb[by]], W=mtPb[mi])
                                if i == 1:
                                    emit(DVE, lambda e, mi=mi, dd=dd: e.tensor_tensor(macc4[dd], macc4[dd], mtP[mi], ALU.add), R=mtPb[mi], W=macc4b[dd])
                                else:
                                    emit(DVE, lambda e, mi=mi, dd=dd, d=d: e.tensor_tensor(mrg[:, d, :], macc4[dd], mtP[mi], ALU.add),
                                         R=mtPb[mi] + macc4b[dd], W=mrgb)
                for eh in range(2):
                    wo = wload(w_out[l, :, eh * 512:eh * 512 + 512].rearrange("(k p) c -> p k c", p=128), 8, 512)
                    for ee in range(4):
                        e_ = eh * 4 + ee
                        b = ps_next()
                        mm(ps[b][:, :], [(wr[wo][:, d, ee * 128:(ee + 1) * 128], mrg[:, d, :]) for d in range(8)],
                           R=[wrb[wo]] + mrgb, W=[psb[b]])
                        emit(ACT, lambda e, e_=e_, b=b: e.activation(out=z3[:, e_, :], in_=ps[b][:, :], func=AF.Copy),
                             R=[psb[b]], W=[ztb])
                        emit(ACT, lambda e, e_=e_, b=b: e.activation(out=xn[:, e_, :], in_=ps[b][:, :], func=AF.Square),
                             R=[psb[b]], W=[xnb])
                postnorm_residual(PP_GPOST + l * 8, 4.0 * EPS, sq_after=True)

                prenorm(PP_GPREF + l * 8, have_sq=True)
                for tt in range(6):
                    w_ = 512 if tt < 5 else 256
                    wa = wload(w_up[l, :, tt * 512:tt * 512 + w_].rearrange("(k p) c -> p k c", p=128), 8, w_)
                    wb = wload(w_up[l, :, DFF + tt * 512:DFF + tt * 512 + w_].rearrange("(k p) c -> p k c", p=128), 8, w_)
                    for jj in range(w_ // 128):
                        j = tt * 4 + jj
                        cs = slice(jj * 128, (jj + 1) * 128)
                        st_ = []
                        for (wt, jh) in ((wa, j), (wb, NJ + j)):
                            hi = ffn_rr[0] % 3
                            ffn_rr[0] += 1
                            hX, hXb, cX, cXb, hXh = hP[hi], hPb[hi], cP[hi], cPb[hi], hPh[hi]
                            b = ps_next()
                            mm(ps[b][:, :], [(wr[wt][:, kc, cs], xn[:, kc, :]) for kc in range(8)], R=[wrb[wt], xnb], W=[psb[b]])
                            emit(ACT, lambda e, hX=hX, b=b: e.activation(out=hX[:, 2:2 + CH], in_=ps[b][:, :], func=AF.Copy),
                                 R=[psb[b]], W=hXb)
                            k0 = PP_CFW + (l * 2 * NJ + jh) * 3
                            emit(ACT, lambda e, cX=cX, k0=k0, b=b: e.activation(out=cX, in_=ps[b][:, :], func=AF.Copy, scale=pp[:, k0 + 2:k0 + 3]),
                                 R=[psb[b], cst], W=cXb)
                            st_.append((hX, hXb, cX, cXb, hXh, k0, jh))
                        gi = ffn_rr[1] % 2
                        ffn_rr[1] += 1
                        for idx, (hX, hXb, cX, cXb, hXh, k0, jh) in enumerate(st_):
                            if c == 0:
                                emit(DVE, lambda e, hX=hX: e.memset(hX[:, 0:2], 0.0), W=[hXh])
                            else:
                                emit(DVE, lambda e, hX=hX, jh=jh: e.tensor_copy(hX[:, 0:2], fha[:, jh, :]), R=[fhab[jh]], W=[hXh])
                            emit(DVE, lambda e, hX=hX, jh=jh: e.tensor_copy(fha[:, jh, :], hX[:, CH:CH + 2]), R=hXb, W=[fhab[jh]])
                            emit(DVE, lambda e, hX=hX, cX=cX, k0=k0: e.scalar_tensor_tensor(out=cX, in0=hX[:, 1:1 + CH], scalar=pp[:, k0 + 1:k0 + 2],
                                                                                          in1=cX, op0=ALU.mult, op1=ALU.add),
                                 R=hXb + [cst, hXh], W=cXb)
                            emit(DVE, lambda e, hX=hX, cX=cX, k0=k0: e.scalar_tensor_tensor(out=cX, in0=hX[:, 0:CH], scalar=pp[:, k0:k0 + 1],
                                                                                          in1=cX, op0=ALU.mult, op1=ALU.add),
                                 R=hXb + [cst, hXh], W=cXb)
                            if idx == 0:
                                emit(ACT, lambda e, cX=cX, gi=gi: e.activation(out=gP[gi], in_=cX, func=AF.Gelu_apprx_tanh), R=cXb, W=gPb[gi])
                        emit(POOL, lambda e, j=j, gi=gi, cB=st_[1][2]: e.tensor_tensor(gff[:, j, :], gP[gi], cB, ALU.mult),
                             R=gPb[gi] + st_[1][3], W=gffb(j))
                for eh in range(2):
                    bks = [ps_next() for _ in range(4)]
                    fns = []
                    Rl = []
                    for kg in range(3):
                        nk = 8 if kg < 2 else 6
                        wd = wload(w_dn[l, kg * 1024:kg * 1024 + nk * 128, eh * 512:eh * 512 + 512].rearrange("(k p) c -> p k c", p=128), nk, 512)
                        fns = []
                        for k in range(nk):
                            j = kg * 8 + k
                            for ee in range(4):
                                fns.append(lambda e, wd=wd, k=k, ee=ee, j=j, bk=bks[ee]: e.matmul(
                                    ps[bk][:, :], wr[wd][:, k, ee * 128:(ee + 1) * 128], gff[:, j, :],
                                    start=(j == 0), stop=(j == NJ - 1)))
                        emit(PE, fns, R=[wrb[wd]] + slot[0:11], W=[psb[bk] for bk in bks])
                    for ee in range(4):
                        e_ = eh * 4 + ee
                        emit(ACT, lambda e, e_=e_, bk=bks[ee]: e.activation(out=z3[:, e_, :], in_=ps[bk][:, :], func=AF.Copy),
                             R=[psb[bks[ee]]], W=[ztb])
                        emit(ACT, lambda e, e_=e_, bk=bks[ee]: e.activation(out=xn[:, e_, :], in_=ps[bk][:, :], func=AF.Square),
                             R=[psb[bks[ee]]], W=[xnb])
                postnorm_residual(PP_GPOSTF + l * 8, EPS)

                if l == depth - 1:
                    for tb in range(4):
                        for half in range(2):
                            b = ps_next()
                            emit(PE, [(lambda e, kq=kq, tb=tb, half=half, b=b: e.transpose(
                                ps[b][:, kq * 128:(kq + 1) * 128], xs[:, half * 4 + kq, tb * 128:(tb + 1) * 128], ident)) for kq in range(4)],
                                 R=[xsb, cst], W=[psb[b]])
                            emit(ACT if half else DVE,
                                 (lambda e, tb=tb, half=half, b=b: e.activation(out=xtok[:, tb, half * 512:(half + 1) * 512], in_=ps[b][:, :], func=AF.Copy))
                                 if half else
                                 (lambda e, tb=tb, half=half, b=b: e.tensor_copy(xtok[:, tb, half * 512:(half + 1) * 512], ps[b][:, :])),
                                 R=[psb[b]], W=[ztb])
                    emit(XST, lambda e, t0=t0: e.dma_start(out=y_d[t0:t0 + CH, :].rearrange("(tb p) d -> p tb d", p=128), in_=xtok),
                         R=[ztb], W=[yb])
                else:
                    emit(XST, lambda e, t0=t0: e.dma_start(out=xT_d[:, :, t0:t0 + CH].rearrange("k p t -> p k t"), in_=xs[:, :, :]),
                         R=[xsb], W=[xTb[c]])

        fin_waits = [(XST.sems[ep], (XST.cnt if ep == XST.ep else None)) for ep in range(len(XST.sems))]
        fin_waits = [(s_, n) for s_, n in fin_waits if n is not None]
        P.ops["sp"].append((fin_waits, [], None, 0))

        with nc.Block() as block:
            def run(stream):
                def f(e):
                    for waits, fns, sem, inc in P.ops[stream]:
                        for s_, n in waits:
                            e.wait_ge(s_, n)
                        ins = None
                        for fn in fns:
                            ins = fn(e)
                        if ins is not None:
                            ins.then_inc(sem, inc)
                return f
            block.tensor(run("pe"))
            block.scalar(run("act"))
            block.vector(run("dve"))
            block.gpsimd(run("pool"))
            block.sync(run("sp"))
    return nc


def _host_layout(pre_mix_g, post_mix_g, pre_ffn_g, post_ffn_g, b_forget, b_gate, conv_mix_w, sgu_ln_g, sgu_ln_b,
                 sgu_w, sgu_b, conv_ffn_w):
    f = np.float32
    pp = np.zeros((128, PP_N), f)
    for off, g in ((PP_GPRE, pre_mix_g), (PP_GPOST, post_mix_g), (PP_GPREF, pre_ffn_g), (PP_GPOSTF, post_ffn_g)):
        pp[:, off:off + 32] = np.asarray(g, f).reshape(L, 8, 128).transpose(2, 0, 1).reshape(128, 32)
    pp[:, PP_BG:PP_BG + 96] = np.asarray(b_gate, f).reshape(L, 3, 8, 128).transpose(3, 0, 1, 2).reshape(128, 96)
    pp[:, PP_CMW:PP_CMW + 24] = np.asarray(conv_mix_w, f).reshape(L, 3, 2, 128).transpose(3, 0, 2, 1).reshape(128, 24)
    pp[:, PP_LNG:PP_LNG + 8] = np.asarray(sgu_ln_g, f).reshape(L, 2, 128).transpose(2, 0, 1).reshape(128, 8)
    pp[:, PP_LNB:PP_LNB + 8] = np.asarray(sgu_ln_b, f).reshape(L, 2, 128).transpose(2, 0, 1).reshape(128, 8)
    pp[:, PP_CFW:PP_CFW + 528] = np.asarray(conv_ffn_w, f).reshape(L, 3, 2 * NJ, 128).transpose(3, 0, 2, 1).reshape(128, 528)
    pp[0:8, PP_BF:PP_BF + 4] = np.asarray(b_forget, f).T
    cc = np.zeros((128, CC_N), f)
    cc[:, CC_ID:CC_ID + 128] = np.eye(128, dtype=f)
    s_, t_ = np.meshgrid(np.arange(128), np.arange(128), indexing="ij")
    cc[:, CC_TRI:CC_TRI + 128] = (t_ >= s_).astype(f)
    cc[:, CC_NEG:CC_NEG + 128] = np.where(t_ < s_, -30000.0, 0.0).astype(f)
    sel = np.zeros((8, 1024), f)
    for h in range(8):
        sel[h, h * 128:(h + 1) * 128] = 1.0
    swT = np.ascontiguousarray(np.asarray(sgu_w, f).transpose(3, 0, 1, 2).reshape(128, L * 4 * 128))
    sb_ = np.asarray(sgu_b, f).reshape(L, 2, 2, 1, 128)
    sb_ = np.broadcast_to(sb_, (L, 2, 2, 64, 128)).transpose(2, 3, 0, 1, 4).reshape(128, L * 2 * 128)
    return pp, cc, swT, np.ascontiguousarray(sb_), sel


_NC_CACHE = {}


def kernel(x, pre_mix_g, post_mix_g, pre_ffn_g, post_ffn_g, w_in, b_forget, b_gate, conv_mix_w, sgu_ln_g, sgu_ln_b,
           sgu_w, sgu_b, w_branch_att, w_branch_conv, w_branch_sgu, w_out, w_ffn_up, conv_ffn_w, w_ffn_down,
           _depth=L, _nch=NCH):
    f = np.float32
    x = np.asarray(x, f)
    B = x.shape[0]
    pp, cc, swT, sgb, sel = _host_layout(pre_mix_g, post_mix_g, pre_ffn_g, post_ffn_g, b_forget, b_gate, conv_mix_w,
                                    sgu_ln_g, sgu_ln_b, sgu_w, sgu_b, conv_ffn_w)
    key = (_depth, _nch)
    if key not in _NC_CACHE:
        _NC_CACHE[key] = build(_depth, _nch)
    nc = _NC_CACHE[key]
    shared = {
        "w_in": np.ascontiguousarray(w_in, f), "w_att": np.ascontiguousarray(w_branch_att, f),
        "w_cv": np.ascontiguousarray(w_branch_conv, f), "w_sg": np.ascontiguousarray(w_branch_sgu, f),
        "w_out": np.ascontiguousarray(w_out, f), "w_up": np.ascontiguousarray(w_ffn_up, f),
        "w_dn": np.ascontiguousarray(w_ffn_down, f), "pp": pp, "cc": cc, "swT": swT, "sgb": sgb, "sel": sel,
    }
    in_maps = [dict(shared, x=np.ascontiguousarray(x[b])) for b in range(B)]
    res = run_bass_kernel_spmd(nc, in_maps, core_ids=list(range(B)))
    return np.stack([np.asarray(r["y"], f) for r in res.results], axis=0)
```

```python
import numpy as np
from contextlib import ExitStack
import concourse.bass as bass
import concourse.mybir as mybir
from concourse.bass_utils import run_bass_kernel_spmd

F32 = mybir.dt.float32
BF = mybir.dt.bfloat16
AF = mybir.ActivationFunctionType
ALU = mybir.AluOpType

L = 4
D = 1024
S = 4096
CH = 512
NCH = S // CH
DFF = 2816
NJ = DFF // 128
INW = 5896
NB = S // 128
EPS = 1e-6
LN_EPS = 1e-5
SLOT = 516
NSLOT = 20
NRING = 4

C_Q, C_K, C_V, C_F, C_BG, C_CG, C_HC, C_U, C_VS, C_G = 0, 512, 1024, 1536, 1544, 1800, 2056, 2312, 2568, 2824

PP_GPRE, PP_GPOST, PP_GPREF, PP_GPOSTF = 0, 32, 64, 96
PP_BG = 128
PP_CMW = 224
PP_LNG = 248
PP_LNB = 256
PP_CFW = 264
PP_BF = 792
PP_N = 796

CC_ID = 0
CC_TRI = 128
CC_NEG = 256
CC_N = 384


class Buf:
    __slots__ = ("w", "r")

    def __init__(self):
        self.w = None
        self.r = {}


class Q:
    def __init__(self, P, stream, inc, safe, name):
        self.P, self.stream, self.inc, self.safe, self.name = P, stream, inc, safe, name
        self.sems = [P.new_sem(name)]
        self.ep = 0
        self.cnt = 0
        self.serial = None if safe else Buf()


class Prog:
    LIMIT = 30000

    def __init__(self, nc, es):
        self.nc, self.es = nc, es
        self.ops = {s: [] for s in ("pe", "act", "dve", "pool", "sp")}
        self.waited = {s: {} for s in self.ops}
        self.nsem = 0

    def new_sem(self, name):
        self.nsem += 1
        return self.es.enter_context(self.nc.semaphore(f"{name}_{self.nsem}"))

    def emit(self, q, fns, R=(), W=()):
        if not isinstance(fns, (list, tuple)):
            fns = [fns]
        W = list(W)
        if q.serial is not None:
            W.append(q.serial)
        deps = {}

        def add(m):
            if m is None:
                return
            k = (m[0], m[1])
            if deps.get(k, 0) < m[2]:
                deps[k] = m[2]

        for b in R:
            add(b.w)
        for b in W:
            add(b.w)
            for m in b.r.values():
                add(m)
        waits = []
        wd = self.waited[q.stream]
        for (dq, ep), n in deps.items():
            if dq is q and q.safe:
                continue
            if wd.get((dq, ep), 0) >= n:
                continue
            wd[(dq, ep)] = n
            waits.append((dq.sems[ep], n))
        if q.cnt + q.inc > self.LIMIT:
            q.ep += 1
            q.cnt = 0
            q.sems.append(self.new_sem(q.name))
        q.cnt += q.inc
        mark = (q, q.ep, q.cnt)
        self.ops[q.stream].append((waits, fns, q.sems[q.ep], q.inc))
        for b in R:
            b.r[q] = mark
        for b in W:
            b.w = mark
            b.r = {}
        return mark


def build(depth=L, nch=NCH):
    nc = bass.Bass("TRN2", target_bir_lowering=False)
    x_d = nc.dram_tensor("x", [S, D], F32, kind="ExternalInput").ap()
    y_d = nc.dram_tensor("y", [S, D], F32, kind="ExternalOutput").ap()
    w_in = nc.dram_tensor("w_in", [L, D, INW], F32, kind="ExternalInput").ap()
    w_att = nc.dram_tensor("w_att", [L, 512, D], F32, kind="ExternalInput").ap()
    w_cv = nc.dram_tensor("w_cv", [L, 256, D], F32, kind="ExternalInput").ap()
    w_sg = nc.dram_tensor("w_sg", [L, 256, D], F32, kind="ExternalInput").ap()
    w_out = nc.dram_tensor("w_out", [L, D, D], F32, kind="ExternalInput").ap()
    w_up = nc.dram_tensor("w_up", [L, D, 2 * DFF], F32, kind="ExternalInput").ap()
    w_dn = nc.dram_tensor("w_dn", [L, DFF, D], F32, kind="ExternalInput").ap()
    pp_d = nc.dram_tensor("pp", [128, PP_N], F32, kind="ExternalInput").ap()
    cc_d = nc.dram_tensor("cc", [128, CC_N], F32, kind="ExternalInput").ap()
    swT_d = nc.dram_tensor("swT", [128, L * 4 * 128], F32, kind="ExternalInput").ap()
    sgb_d = nc.dram_tensor("sgb", [128, L * 2 * 128], F32, kind="ExternalInput").ap()
    sel_d = nc.dram_tensor("sel", [8, 1024], F32, kind="ExternalInput").ap()
    xT_d = nc.dram_tensor("xT", [8, 128, S], F32, kind="Internal").ap()
    NT = 35
    wbf_d = nc.dram_tensor("wbf", [L, NT, 128, 4096], BF, kind="Internal").ap()

    es = ExitStack()
    with es:
        def sb(name, shape, dt):
            return es.enter_context(nc.sbuf_tensor("s_" + name, shape, dt))

        xs = sb("xs", [128, 8, CH], F32)
        xn = sb("xn", [128, 8, CH], BF)
        zt = sb("zt", [128, 8 * CH], F32)
        KT = sb("KT", [128, 4, S], BF)
        VA = sb("VA", [128, NB, 8, 65], BF)
        negc = sb("negc", [128, NB, 8], F32)
        arena = sb("arena", [128, NSLOT * SLOT], F32)
        st = [sb(f"st{i}", [128, CH], F32) for i in range(3)]
        pT = [sb(f"pT{i}", [128, CH], BF) for i in range(3)]
        cT = sb("cT", [8, CH], F32)
        mT = sb("mT", [128, CH], BF)
        carry = sb("carry", [8, 1], F32)
        ones8 = sb("ones8", [8, CH], BF)
        rinv = sb("rinv", [128, 4], F32)
        wr = [sb(f"wr{i}", [128, 8, 512], BF) for i in range(NRING)]
        pp = sb("pp", [128, PP_N], F32)
        hbg = sb("hbg", [128, 96], F32)
        nbf = sb("nbf", [8, 4], F32)
        cc = sb("cc", [128, CC_N], F32)
        identb = sb("identb", [128, 128], BF)
        negmb = sb("negmb", [128, 128], BF)
        selb = sb("selb", [128, 1024], BF)
        onesb = sb("onesb", [128, 128], BF)
        WmT = sb("WmT", [128, L * 4 * 128], BF)
        sgb = sb("sgbs", [128, L * 2 * 128], F32)
        cvh = sb("cvh", [128, 2, 2], F32)
        fha = sb("fha", [128, 2 * NJ, 2], F32)
        ps = [es.enter_context(nc.psum_tensor(f"ps{i}", [128, 512], F32)) for i in range(8)]

        P = Prog(nc, es)
        PE = Q(P, "pe", 1, True, "pe")
        ACT = Q(P, "act", 1, False, "act")
        DVE = Q(P, "dve", 1, False, "dve")
        POOL = Q(P, "pool", 1, False, "pool")
        RQ = [Q(P, "sp", 16, False, f"rq{i}") for i in range(NRING)]
        CVQ = [Q(P, "pool", 16, False, f"cvq{i}") for i in range(4)]
        XLD = Q(P, "pool", 16, False, "xld")
        XST = Q(P, "pool", 16, False, "xst")
        CQ = Q(P, "sp", 16, False, "cq")
        emit = P.emit

        xsb, xnb, ztb = Buf(), Buf(), Buf()
        KTb = [Buf() for _ in range(NCH)]
        VAb = [Buf() for _ in range(NCH)]
        ngb = [Buf() for _ in range(NCH)]
        slot = [Buf() for _ in range(NSLOT)]
        stb = [Buf() for _ in range(4)]
        pTb = [Buf(), Buf(), Buf()]
        cTb, mTb, carb, rinvb = Buf(), Buf(), Buf(), Buf()
        spT = st[1][0:8, :]
        spTb = stb[1]
        wrb = [Buf() for _ in range(NRING)]
        cst = Buf()
        cvhb, fhab = Buf(), [Buf() for _ in range(2 * NJ)]
        psb = [Buf() for _ in range(8)]
        xTb = [Buf() for _ in range(NCH)]
        yb = Buf()

        def av(s0, n, dt, dims):
            ap = arena[:, s0 * SLOT:(s0 + n) * SLOT]
            if dt is BF:
                ap = ap.bitcast(BF)
            tot = 1
            for d_ in dims:
                tot *= d_
            ap = ap[:, 0:tot]
            if len(dims) == 2:
                ap = ap.rearrange("p (a b) -> p a b", a=dims[0])
            return ap

        QT = av(16, 4, BF, [8, CH]); QTb = slot[16:20]
        ycv = av(2, 1, BF, [2, CH]); ycvb = slot[2:3]
        ysg = av(3, 1, BF, [2, CH]); ysgb = slot[3:4]
        gu = av(4, 1, BF, [2, CH]); gub = slot[4:5]
        pbuf = [av(5, 1, F32, [SLOT]), av(6, 1, F32, [SLOT])]; pbufb = [slot[5:6], slot[6:7]]
        tmph = av(7, 1, F32, [CH]); tmphb = slot[7:8]
        cacc = av(8, 1, F32, [CH]); caccb = slot[8:9]
        gv = av(9, 2, F32, [2, CH]); gvb_ = slot[9:11]
        vn = av(11, 2, F32, [2, CH]); vnb = slot[11:13]
        gvh = av(13, 1, BF, [2, CH]); gvhb = slot[13:14]
        gv2 = av(14, 1, BF, [2, CH]); gv2b = slot[14:15]
        vtok = av(15, 1, BF, [4, 256]); vtokb = slot[15:16]
        atok = av(5, 4, F32, [4, 512]); atokb = slot[5:9]
        attT = av(9, 2, BF, [4, CH]); attTb = slot[9:11]
        th = av(11, 1, F32, [CH]); thb = slot[11:12]
        macc = av(12, 1, F32, [CH]); maccb = slot[12:13]
        mtmp = av(14, 1, F32, [CH]); mtmpb = slot[14:15]
        mrg = av(5, 4, BF, [8, CH]); mrgb = slot[5:9]
        gff = av(0, 11, BF, [NJ, CH])
        def gffb(j):
            return slot[(j * CH) // (2 * SLOT):((j + 1) * CH - 1) // (2 * SLOT) + 1]
        hP = [av(11 + i, 1, F32, [SLOT]) for i in range(3)]; hPb = [slot[11 + i:12 + i] for i in range(3)]
        cP = [av(14 + i, 1, F32, [CH]) for i in range(3)]; cPb = [slot[14 + i:15 + i] for i in range(3)]
        gP = [av(17 + i, 1, F32, [CH]) for i in range(2)]; gPb = [slot[17 + i:18 + i] for i in range(2)]
        hPh = [Buf() for _ in range(3)]
        ffn_rr = [0, 0]
        thP = [av(i, 1, F32, [CH]) for i in (11, 13, 15)]; thPb = [slot[i:i + 1] for i in (11, 13, 15)]
        mtP = [av(i, 1, F32, [CH]) for i in (14, 0)]; mtPb = [slot[i:i + 1] for i in (14, 0)]
        macc4 = [av(i, 1, F32, [CH]) for i in (12, 16, 17, 18)]; macc4b = [slot[i:i + 1] for i in (12, 16, 17, 18)]
        mrg_rr = [0, 0]

        xpre = arena[:, 11 * SLOT:11 * SLOT + 8 * CH]
        xpre8 = xpre.rearrange("p (a b) -> p a b", a=8)
        xpre4 = xpre.rearrange("p (a b) -> p a b", a=4)
        xpreb = slot[11:19]
        prefetched = set()
        z3 = zt[:, :].rearrange("p (a b) -> p a b", a=8)
        xtok = zt[:, :].rearrange("p (a b) -> p a b", a=4)

        ps_rr = [0]

        def ps_next(exclude=()):
            while True:
                i = ps_rr[0]
                ps_rr[0] = (i + 1) % 8
                if i not in exclude:
                    return i

        ring_rr = [0]

        wl_cnt = [0]
        cur_l = [0]
        wcvb = {}
        cv_rr = [0]

        def wbf3(l_, tid):
            return wbf_d[l_, tid].rearrange("p (k c) -> p k c", k=8)

        def tile_specs(l_):
            Rr = lambda ap: ap.rearrange("(k p) c -> p k c", p=128)
            sp_ = []
            for c0, w in ((C_Q, 512), (C_K, 512), (C_V, 512), (C_F, 8), (C_BG, 512), (C_HC, 512), (C_VS, 256)):
                sp_.append([(Rr(w_in[l_, :, c0:c0 + w]), 0, 8, w)])
            for dh in range(2):
                sp_.append([(Rr(w_att[l_, :, dh * 512:dh * 512 + 512]), 0, 4, 512),
                            (Rr(w_cv[l_, :, dh * 512:dh * 512 + 512]), 4, 2, 512),
                            (Rr(w_sg[l_, :, dh * 512:dh * 512 + 512]), 6, 2, 512)])
                for i in range(3):
                    sp_.append([(Rr(w_in[l_, :, C_G + i * D + dh * 512:C_G + i * D + dh * 512 + 512]), 0, 8, 512)])
            for eh in range(2):
                sp_.append([(Rr(w_out[l_, :, eh * 512:eh * 512 + 512]), 0, 8, 512)])
            for tt in range(6):
                w_ = 512 if tt < 5 else 256
                sp_.append([(Rr(w_up[l_, :, tt * 512:tt * 512 + w_]), 0, 8, w_)])
                sp_.append([(Rr(w_up[l_, :, DFF + tt * 512:DFF + tt * 512 + w_]), 0, 8, w_)])
            for eh in range(2):
                for kg in range(3):
                    nk = 8 if kg < 2 else 6
                    sp_.append([(Rr(w_dn[l_, kg * 1024:kg * 1024 + nk * 128, eh * 512:eh * 512 + 512]), 0, nk, 512)])
            assert len(sp_) == NT
            return sp_

        def conv_tile(l_, tid, parts):
            b_ = wcvb.setdefault((l_, tid), Buf())
            for (src, k0, nk, w) in parts:
                q_ = CVQ[cv_rr[0] % 4]
                cv_rr[0] += 1
                emit(q_, lambda e, src=src, k0=k0, nk=nk, w=w: e.dma_start(out=wbf3(l_, tid)[:, k0:k0 + nk, 0:w], in_=src), W=[b_])

        def wload(src_ap, nk, w):
            tid = wl_cnt[0]
            wl_cnt[0] += 1
            l_ = cur_l[0]
            i = ring_rr[0]
            ring_rr[0] = (i + 1) % NRING
            emit(RQ[i], lambda e: e.dma_start(out=wr[i][:, 0:nk, 0:w], in_=wbf3(l_, tid)[:, 0:nk, 0:w]), R=[wcvb[(l_, tid)]], W=[wrb[i]])
            return i

        def mm(out_ap, pairs, R, W, extra=()):
            n = len(pairs)
            fns = [(lambda e, a=a, b=b, i=i: e.matmul(out_ap, a, b, start=(i == 0), stop=(i == n - 1 and not extra)))
                   for i, (a, b) in enumerate(pairs)]
            fns += list(extra)
            emit(PE, fns, R, W)

        emit(CQ, lambda e: e.dma_start(out=pp[:, :], in_=pp_d[:, :]), W=[cst])
        emit(CQ, lambda e: e.dma_start(out=cc[:, :], in_=cc_d[:, :]), W=[cst])
        emit(CQ, lambda e: e.dma_start(out=zt[:, 0:L * 512], in_=swT_d[:, :]), W=[cst, ztb])
        emit(CQ, lambda e: e.dma_start(out=sgb[:, :], in_=sgb_d[:, :]), W=[cst])
        emit(DVE, lambda e: e.tensor_copy(identb[:, :], cc[:, CC_ID:CC_ID + 128]), R=[cst], W=[cst])
        emit(DVE, lambda e: e.tensor_copy(negmb[:, :], cc[:, CC_NEG:CC_NEG + 128]), R=[cst], W=[cst])
        emit(CQ, lambda e: e.dma_start(out=zt[0:8, 2048:3072], in_=sel_d[:, :]), W=[cst, ztb])
        emit(DVE, lambda e: e.memset(selb[:, :], 0.0), W=[cst])
        emit(DVE, lambda e: e.memset(mT[:, :], 0.0), W=[mTb])
        emit(DVE, lambda e: e.tensor_copy(selb[0:8, :], zt[0:8, 2048:3072]), R=[cst, ztb], W=[cst])
        emit(DVE, lambda e: e.memset(onesb[:, :], 1.0), W=[cst])
        emit(DVE, lambda e: e.memset(ones8[:, :], 1.0), W=[cst])
        emit(DVE, lambda e: e.memset(VA[:, :, :, 64:65], 1.0), W=VAb)
        emit(DVE, lambda e: e.tensor_scalar(hbg[:, :], pp[:, PP_BG:PP_BG + 96], 0.5, None, ALU.mult), R=[cst], W=[cst])
        emit(DVE, lambda e: e.tensor_scalar(nbf[:, :], pp[0:8, PP_BF:PP_BF + 4], -1.0, None, ALU.mult), R=[cst], W=[cst])
        for l in range(depth):
            for g in range(4):
                o = (l * 4 + g) * 128
                emit(DVE, lambda e, o=o: e.tensor_tensor(WmT[:, o:o + 128], zt[:, o:o + 128],
                                                       cc[:, CC_TRI:CC_TRI + 128], ALU.mult), R=[cst, ztb], W=[cst])
        ident = cc[:, CC_ID:CC_ID + 128]

        def rms_stats(src3, srcb, eps, mean_scale, have_sq=False):
            if not have_sq:
                emit(ACT, lambda e: e.activation(out=xn[:, :, :], in_=src3, func=AF.Square), R=srcb, W=[xnb])
            b = ps_next()
            mm(ps[b][:, :], [(onesb[:, :], xn[:, kc, :]) for kc in range(8)], R=[xnb, cst], W=[psb[b]])
            emit(ACT, lambda e: e.activation(out=st[0][:, :], in_=ps[b][:, :], func=AF.Ln, bias=eps, scale=mean_scale),
                 R=[psb[b]], W=[stb[0]])
            emit(ACT, lambda e: e.activation(out=st[0][:, :], in_=st[0][:, :], func=AF.Exp, scale=-0.5),
                 R=[stb[0]], W=[stb[0]])

        def prenorm(goff, have_sq=False):
            rms_stats(xs[:, :, :], [xsb], EPS, 1.0 / D, have_sq)
            for kc in range(8):
                emit(DVE, lambda e, kc=kc: e.scalar_tensor_tensor(out=xn[:, kc, :], in0=xs[:, kc, :],
                                                                 scalar=pp[:, goff + kc:goff + kc + 1], in1=st[0][:, :],
                                                                 op0=ALU.mult, op1=ALU.mult),
                     R=[xsb, stb[0], cst], W=[xnb])

        def postnorm_residual(goff, eps, sq_after=False):
            rms_stats(z3, [ztb], eps, 1.0 / D, True)
            for kc in range(8):
                emit(DVE, lambda e, kc=kc: e.scalar_tensor_tensor(out=z3[:, kc, :], in0=z3[:, kc, :],
                                                                 scalar=pp[:, goff + kc:goff + kc + 1], in1=st[0][:, :],
                                                                 op0=ALU.mult, op1=ALU.mult),
                     R=[stb[0], cst], W=[ztb])
            for kc in range(8):
                emit(DVE, lambda e, kc=kc: e.tensor_tensor(xs[:, kc, :], xs[:, kc, :], z3[:, kc, :], ALU.add), R=[ztb], W=[xsb])
                if sq_after:
                    emit(ACT, lambda e, kc=kc: e.activation(out=xn[:, kc, :], in_=xs[:, kc, :], func=AF.Square), R=[xsb], W=[xnb])

        specs = [tile_specs(l_) for l_ in range(depth)]
        for tid in range(NT):
            conv_tile(0, tid, specs[0][tid])
        for l in range(depth):
            for c in range(nch):
                t0 = c * CH
                wl_cnt[0] = 0
                cur_l[0] = l
                pre = (l, c) in prefetched
                if l == 0:
                    if not pre:
                        emit(XLD, lambda e, t0=t0: e.dma_start(
                            out=xtok, in_=x_d[t0:t0 + CH, :].rearrange("(tb p) d -> p tb d", p=128)), W=[ztb])
                    srcv, srcb_ = (xpre4, xpreb) if pre else (xtok, [ztb])
                    for kc in range(8):
                        b = ps_next()
                        emit(PE, [(lambda e, tb=tb, kc=kc, b=b, srcv=srcv: e.transpose(ps[b][:, tb * 128:(tb + 1) * 128],
                                                                          srcv[:, tb, kc * 128:(kc + 1) * 128], ident))
                                  for tb in range(4)], R=srcb_ + [cst], W=[psb[b]])
                        emit(ACT if kc % 2 else DVE,
                             (lambda e, kc=kc, b=b: e.activation(out=xs[:, kc, :], in_=ps[b][:, :], func=AF.Copy)) if kc % 2
                             else (lambda e, kc=kc, b=b: e.tensor_copy(xs[:, kc, :], ps[b][:, :])),
                             R=[psb[b]], W=[xsb])
                elif pre:
                    emit(ACT, lambda e: e.activation(out=xs[:, 0:4, :], in_=xpre8[:, 0:4, :], func=AF.Copy), R=xpreb, W=[xsb])
                    emit(DVE, lambda e: e.tensor_copy(xs[:, 4:8, :], xpre8[:, 4:8, :]), R=xpreb, W=[xsb])
                else:
                    emit(XLD, lambda e, t0=t0: e.dma_start(
                        out=xs[:, :, :], in_=xT_d[:, :, t0:t0 + CH].rearrange("k p t -> p k t")), R=[xTb[c]], W=[xsb])

                prenorm(PP_GPRE + l * 8)

                wi = wload(w_in[l, :, C_Q:C_Q + 512].rearrange("(k p) c -> p k c", p=128), 8, 512)
                for fc in range(4):
                    b = ps_next()
                    mm(ps[b][:, :], [(wr[wi][:, kc, fc * 128:(fc + 1) * 128], xn[:, kc, :]) for kc in range(8)],
                       R=[wrb[wi], xnb], W=[psb[b]])
                    if fc == 0:
                        emit(DVE, lambda e: e.memset(QT, 0.0), W=QTb)
                    for hh in range(2):
                        emit(ACT, lambda e, fc=fc, b=b, hh=hh: e.activation(out=QT[hh * 64:hh * 64 + 64, 2 * fc + hh, :],
                                                                          in_=ps[b][hh * 64:hh * 64 + 64, :], func=AF.Copy, scale=0.125),
                             R=[psb[b]], W=QTb)
                wi = wload(w_in[l, :, C_K:C_K + 512].rearrange("(k p) c -> p k c", p=128), 8, 512)
                for fc in range(4):
                    b = ps_next()
                    mm(ps[b][:, :], [(wr[wi][:, kc, fc * 128:(fc + 1) * 128], xn[:, kc, :]) for kc in range(8)],
                       R=[wrb[wi], xnb], W=[psb[b]])
                    emit(DVE, lambda e, fc=fc, b=b, t0=t0: e.tensor_copy(KT[:, fc, t0:t0 + CH], ps[b][:, :]),
                         R=[psb[b]], W=[KTb[c]])
                wi = wload(w_in[l, :, C_V:C_V + 512].rearrange("(k p) c -> p k c", p=128), 8, 512)
                for tb in range(4):
                    b = ps_next()
                    mm(ps[b][:, :], [(xn[:, kc, tb * 128:(tb + 1) * 128], wr[wi][:, kc, 0:512]) for kc in range(8)],
                       R=[wrb[wi], xnb], W=[psb[b]])
                    emit(ACT if tb % 2 else DVE,
                         (lambda e, tb=tb, b=b, c=c: e.activation(out=VA[:, c * 4 + tb, :, 0:64],
                                                              in_=ps[b][:, :].rearrange("p (h d) -> p h d", h=8), func=AF.Copy))
                         if tb % 2 else
                         (lambda e, tb=tb, b=b, c=c: e.tensor_copy(VA[:, c * 4 + tb, :, 0:64],
                                                               ps[b][:, :].rearrange("p (h d) -> p h d", h=8))),
                         R=[psb[b]], W=[VAb[c]])
                wi = wload(w_in[l, :, C_F:C_F + 8].rearrange("(k p) c -> p k c", p=128), 8, 8)
                b = ps_next()
                mm(ps[b][0:8, :], [(wr[wi][:, kc, 0:8], xn[:, kc, :]) for kc in range(8)], R=[wrb[wi], xnb], W=[psb[b]])
                emit(ACT, lambda e, b=b, l=l: e.activation(out=spT[:, :], in_=ps[b][0:8, :], func=AF.Exp,
                                                       bias=nbf[:, l:l + 1], scale=-1.0), R=[psb[b], cst], W=[spTb])
                emit(ACT, lambda e: e.activation(out=spT[:, :], in_=spT[:, :], func=AF.Ln, bias=1.0, scale=1.0),
                     R=[spTb], W=[spTb])
                if c == 0:
                    emit(DVE, lambda e: e.memset(carry[:, :], 0.0), W=[carb])
                emit(DVE, lambda e: e.tensor_tensor_scan(cT[:, :], ones8[:, :], spT[:, :], carry[:, 0:1], ALU.mult, ALU.subtract),
                     R=[spTb, carb, cst], W=[cTb])
                emit(DVE, lambda e: e.tensor_copy(carry[:, 0:1], cT[:, CH - 1:CH]), R=[cTb], W=[carb])
                emit(DVE, lambda e: e.tensor_copy(mT[0:8, :], cT[:, :]), R=[cTb], W=[mTb])
                b = ps_next()
                emit(PE, [(lambda e, tb=tb, b=b: e.matmul(ps[b][:, tb * 8:(tb + 1) * 8], cT[:, tb * 128:(tb + 1) * 128],
                                                        cc[0:8, CC_ID:CC_ID + 8], start=True, stop=True)) for tb in range(4)],
                     R=[cTb, cst], W=[psb[b]])
                emit(ACT, lambda e, b=b, c=c: e.activation(out=negc[:, c * 4:(c + 1) * 4, :],
                                                       in_=ps[b][:, 0:32].rearrange("p (a h) -> p a h", a=4),
                                                       func=AF.Copy, scale=-1.0), R=[psb[b]], W=[ngb[c]])
                wa = wload(w_in[l, :, C_BG:C_BG + 512].rearrange("(k p) c -> p k c", p=128), 8, 512)
                wb = wload(w_in[l, :, C_HC:C_HC + 512].rearrange("(k p) c -> p k c", p=128), 8, 512)
                wv = wload(w_in[l, :, C_VS:C_VS + 256].rearrange("(k p) c -> p k c", p=128), 8, 256)
                for j in range(2):
                    b = ps_next()
                    mm(ps[b][:, :], [(wr[wb][:, kc, 256 + j * 128:256 + (j + 1) * 128], xn[:, kc, :]) for kc in range(8)],
                       R=[wrb[wb], xnb], W=[psb[b]])
                    emit(ACT, lambda e, j=j, b=b: e.activation(out=gu[:, j, :], in_=ps[b][:, :], func=AF.Gelu_apprx_tanh),
                         R=[psb[b]], W=gub)
                    b = ps_next()
                    mm(ps[b][:, :], [(wr[wv][:, kc, j * 128:(j + 1) * 128], xn[:, kc, :]) for kc in range(8)],
                       R=[wrb[wv], xnb], W=[psb[b]])
                    emit(ACT, lambda e, j=j, b=b: e.activation(out=gv[:, j, :], in_=ps[b][:, :], func=AF.Gelu_apprx_tanh),
                         R=[psb[b]], W=gvb_)
                emit(DVE, lambda e: e.tensor_copy(gvh, gv), R=gvb_, W=gvhb)
                emit(ACT, lambda e: e.activation(out=gv2, in_=gv, func=AF.Square), R=gvb_, W=gv2b)
                b1 = ps_next()
                mm(ps[b1][:, :], [(onesb[:, :], gvh[:, j, :]) for j in range(2)], R=gvhb + [cst], W=[psb[b1]])
                b2 = ps_next()
                mm(ps[b2][:, :], [(onesb[:, :], gv2[:, j, :]) for j in range(2)], R=gv2b + [cst], W=[psb[b2]])
                emit(ACT, lambda e, b1=b1: e.activation(out=st[1][:, :], in_=ps[b1][:, :], func=AF.Copy, scale=1.0 / 256),
                     R=[psb[b1]], W=[stb[1]])
                emit(DVE, lambda e: e.tensor_tensor(st[2][:, :], st[1][:, :], st[1][:, :], ALU.mult), R=[stb[1]], W=[stb[2]])
                emit(DVE, lambda e, b2=b2: e.scalar_tensor_tensor(out=st[2][:, :], in0=ps[b2][:, :], scalar=1.0 / 256, in1=st[2][:, :],
                                                                op0=ALU.mult, op1=ALU.subtract), R=[psb[b2], stb[2]], W=[stb[2]])
                emit(ACT, lambda e: e.activation(out=st[2][:, :], in_=st[2][:, :], func=AF.Ln, bias=LN_EPS, scale=1.0),
                     R=[stb[2]], W=[stb[2]])
                emit(ACT, lambda e: e.activation(out=st[2][:, :], in_=st[2][:, :], func=AF.Exp, scale=-0.5),
                     R=[stb[2]], W=[stb[2]])
                for j in range(2):
                    emit(DVE, lambda e, j=j: e.tensor_tensor(vn[:, j, :], gv[:, j, :], st[1][:, :], ALU.subtract),
                         R=gvb_ + [stb[1]], W=vnb)
                    emit(DVE, lambda e, j=j: e.tensor_tensor(vn[:, j, :], vn[:, j, :], st[2][:, :], ALU.mult),
                         R=[stb[2]], W=vnb)
                    og, ob = PP_LNG + l * 2 + j, PP_LNB + l * 2 + j
                    emit(DVE, lambda e, j=j, og=og, ob=ob: e.tensor_scalar(vn[:, j, :], vn[:, j, :], pp[:, og:og + 1], pp[:, ob:ob + 1],
                                                                         ALU.mult, ALU.add), R=[cst], W=vnb)
                for j in range(2):
                    bh = ps_next()
                    mm(ps[bh][:, :], [(wr[wb][:, kc, j * 128:(j + 1) * 128], xn[:, kc, :]) for kc in range(8)],
                       R=[wrb[wb], xnb], W=[psb[bh]])
                    emit(ACT, lambda e, bh=bh: e.activation(out=tmph, in_=ps[bh][:, :], func=AF.Copy), R=[psb[bh]], W=tmphb)
                    bc_ = ps_next()
                    mm(ps[bc_][:, :], [(wr[wa][:, kc, 256 + j * 128:256 + (j + 1) * 128], xn[:, kc, :]) for kc in range(8)],
                       R=[wrb[wa], xnb], W=[psb[bc_]])
                    pb = pbuf[j]
                    if c == 0:
                        emit(DVE, lambda e, pb=pb: e.memset(pb[:, 0:2], 0.0), W=pbufb[j])
                    else:
                        emit(DVE, lambda e, pb=pb, j=j: e.tensor_copy(pb[:, 0:2], cvh[:, j, :]), R=[cvhb], W=pbufb[j])
                    emit(DVE, lambda e, pb=pb, bc_=bc_: e.tensor_tensor(pb[:, 2:2 + CH], ps[bc_][:, :], tmph, ALU.mult),
                         R=[psb[bc_]] + tmphb, W=pbufb[j])
                    emit(DVE, lambda e, pb=pb, j=j: e.tensor_copy(cvh[:, j, :], pb[:, CH:CH + 2]), R=pbufb[j], W=[cvhb])
                    k0 = PP_CMW + (l * 2 + j) * 3
                    emit(DVE, lambda e, pb=pb, k0=k0: e.tensor_scalar(cacc, pb[:, 2:2 + CH], pp[:, k0 + 2:k0 + 3], None, ALU.mult),
                         R=pbufb[j] + [cst], W=caccb)
                    emit(DVE, lambda e, pb=pb, k0=k0: e.scalar_tensor_tensor(out=cacc, in0=pb[:, 1:1 + CH], scalar=pp[:, k0 + 1:k0 + 2],
                                                                           in1=cacc, op0=ALU.mult, op1=ALU.add),
                         R=pbufb[j] + [cst], W=caccb)
                    emit(DVE, lambda e, pb=pb, k0=k0: e.scalar_tensor_tensor(out=cacc, in0=pb[:, 0:CH], scalar=pp[:, k0:k0 + 1],
                                                                           in1=cacc, op0=ALU.mult, op1=ALU.add),
                         R=pbufb[j] + [cst], W=caccb)
                    bb_ = ps_next()
                    mm(ps[bb_][:, :], [(wr[wa][:, kc, j * 128:(j + 1) * 128], xn[:, kc, :]) for kc in range(8)],
                       R=[wrb[wa], xnb], W=[psb[bb_]])
                    emit(DVE, lambda e, j=j, bb_=bb_: e.tensor_tensor(ycv[:, j, :], ps[bb_][:, :], cacc, ALU.mult),
                         R=[psb[bb_]] + caccb, W=ycvb)
                if l + 1 < depth:
                    per = -(-NT // nch)
                    for tid in range(c * per, min(NT, (c + 1) * per)):
                        conv_tile(l + 1, tid, specs[l + 1][tid])
                nkb = 4 * c + 4
                tiles = [(h, kb) for h in range(8) for kb in range(nkb)]
                obank = {}
                tinfo = {}

                def s_stage(i):
                    h, kb = tiles[i]
                    fc, r0 = h // 2, (h % 2) * 64
                    if h not in obank:
                        obank[h] = ps_next(exclude=tuple(obank.values()))
                    excl = tuple(obank.values())
                    kc_ = kb // 4
                    r = kb - 4 * c
                    bs = ps_next(exclude=excl)
                    kT = KT[:, fc, kb * 128:(kb + 1) * 128]
                    fns = []
                    if r < 0:
                        fns.append(lambda e: e.matmul(ps[bs][:, :], kT, QT[:, h, :], start=True, stop=False))
                        fns.append(lambda e: e.matmul(ps[bs][:, :], selb[:, h * 128:(h + 1) * 128], mT[:, :], start=False, stop=True))
                        q0 = 0
                    else:
                        q0 = r * 128
                        fns.append(lambda e: e.matmul(ps[bs][:, q0:q0 + 128], kT, QT[:, h, q0:q0 + 128], start=True, stop=False))
                        fns.append(lambda e: e.matmul(ps[bs][:, q0:q0 + 128], selb[:, h * 128:(h + 1) * 128], mT[:, q0:q0 + 128], start=False, stop=False))
                        fns.append(lambda e: e.matmul(ps[bs][:, q0:q0 + 128], identb[:, :], negmb[:, :], start=False, stop=True))
                        if q0 + 128 < CH:
                            fns.append(lambda e: e.matmul(ps[bs][:, q0 + 128:CH], kT, QT[:, h, q0 + 128:CH], start=True, stop=False))
                            fns.append(lambda e: e.matmul(ps[bs][:, q0 + 128:CH], selb[:, h * 128:(h + 1) * 128], mT[:, q0 + 128:CH], start=False, stop=True))
                    emit(PE, fns, R=[KTb[kc_], mTb, cst] + QTb, W=[psb[bs]])
                    pi = i % 3
                    emit(ACT, lambda e: e.activation(out=pT[pi][:, q0:CH], in_=ps[bs][:, q0:CH], func=AF.Exp,
                                                     bias=negc[:, kb, h:h + 1], scale=1.0),
                         R=[psb[bs], ngb[kc_]], W=[pTb[pi]])
                    tinfo[i] = (q0, pi, kc_)

                def pv_stage(i):
                    h, kb = tiles[i]
                    q0, pi, kc_ = tinfo.pop(i)
                    bo = obank[h]
                    pso = ps[bo][:, 0:260].rearrange("p (a b) -> p a b", a=4)
                    fns = []
                    for qb in range(q0 // 128, 4):
                        fns.append(lambda e, qb=qb: e.matmul(
                            pso[:, qb, :], pT[pi][:, qb * 128:(qb + 1) * 128], VA[:, kb, h, :],
                            start=(kb == 0 and qb == 0), stop=(kb == 4 * c + qb), skip_group_check=True))
                    emit(PE, fns, R=[pTb[pi], VAb[kc_]], W=[psb[bo]])
                    if kb == nkb - 1:
                        emit(DVE, lambda e: e.reciprocal(rinv[:, :], pso[:, :, 64]), R=[psb[bo]], W=[rinvb])
                        for qb in range(4):
                            emit(DVE, lambda e, qb=qb: e.tensor_scalar(atok[:, qb, h * 64:(h + 1) * 64], pso[:, qb, 0:64],
                                                                      rinv[:, qb:qb + 1], None, ALU.mult),
                                 R=[psb[bo], rinvb], W=atokb)
                        if h - 1 in obank:
                            del obank[h - 1]

                s_stage(0)
                if len(tiles) > 1:
                    s_stage(1)
                for i in range(len(tiles)):
                    if i + 2 < len(tiles):
                        s_stage(i + 2)
                    pv_stage(i)
                for tb in range(4):
                    b = ps_next()
                    emit(PE, [(lambda e, j=j, tb=tb, b=b: e.transpose(ps[b][:, j * 128:(j + 1) * 128],
                                                                    vn[:, j, tb * 128:(tb + 1) * 128], ident)) for j in range(2)],
                         R=vnb + [cst], W=[psb[b]])
                    emit(ACT, lambda e, tb=tb, b=b: e.activation(out=vtok[:, tb, :], in_=ps[b][:, 0:256], func=AF.Copy),
                         R=[psb[b]], W=vtokb)
                for j in range(2):
                    bA, bB = ps_next(), ps_next()
                    fns = []
                    for tb in range(4):
                        for hh, bx in ((0, bA), (1, bB)):
                            o = (l * 4 + 2 * j + hh) * 128
                            fns.append(lambda e, tb=tb, bx=bx, o=o, j=j: e.matmul(
                                ps[bx][:, tb * 128:(tb + 1) * 128], vtok[:, tb, j * 128:(j + 1) * 128], WmT[:, o:o + 128],
                                start=True, stop=True))
                    emit(PE, fns, R=vtokb + [cst], W=[psb[bA], psb[bB]])
                    ob = (l * 2 + j) * 128
                    for hh, bx in ((0, bA), (1, bB)):
                        r0, r1 = hh * 64, hh * 64 + 64
                        for tb in range(4):
                            emit(DVE, lambda e, bx=bx, r0=r0, r1=r1, ob=ob, tb=tb: e.tensor_tensor(
                                th[r0:r1, tb * 128:(tb + 1) * 128], ps[bx][r0:r1, tb * 128:(tb + 1) * 128],
                                sgb[r0:r1, ob:ob + 128], ALU.add), R=[psb[bx], cst], W=thb)
                    emit(DVE, lambda e, j=j: e.tensor_tensor(ysg[:, j, :], th, gu[:, j, :], ALU.mult), R=thb + gub, W=ysgb)

                for fc in range(4):
                    b = ps_next()
                    emit(PE, [(lambda e, qb=qb, fc=fc, b=b: e.transpose(ps[b][:, qb * 128:(qb + 1) * 128],
                                                                      atok[:, qb, fc * 128:(fc + 1) * 128], ident)) for qb in range(4)],
                         R=atokb + [cst], W=[psb[b]])
                    emit(ACT, lambda e, fc=fc, b=b: e.activation(out=attT[:, fc, :], in_=ps[b][:, :], func=AF.Copy),
                         R=[psb[b]], W=attTb)

                for dh in range(2):
                    ib = wload(None, 8, 512)
                    for i in range(3):
                        wgi = wload(w_in[l, :, C_G + i * D + dh * 512:C_G + i * D + dh * 512 + 512].rearrange("(k p) c -> p k c", p=128), 8, 512)
                        for dd in range(4):
                            d = dh * 4 + dd
                            cs = slice(dd * 128, (dd + 1) * 128)
                            if i == 0:
                                src_, srcb_ = [(wr[ib][:, k, cs], attT[:, k, :]) for k in range(4)], attTb
                            elif i == 1:
                                src_, srcb_ = [(wr[ib][:, 4 + k, cs], ycv[:, k, :]) for k in range(2)], ycvb
                            else:
                                src_, srcb_ = [(wr[ib][:, 6 + k, cs], ysg[:, k, :]) for k in range(2)], ysgb
                            ti = mrg_rr[0] % 3
                            mrg_rr[0] += 1
                            bg_ = ps_next()
                            mm(ps[bg_][:, :], [(wr[wgi][:, kc, cs], xn[:, kc, :]) for kc in range(8)],
                               R=[wrb[wgi], xnb], W=[psb[bg_]])
                            ob = l * 24 + i * 8 + d
                            emit(ACT, lambda e, bg_=bg_, ob=ob, ti=ti: e.activation(out=thP[ti], in_=ps[bg_][:, :], func=AF.Tanh,
                                                                                bias=hbg[:, ob:ob + 1], scale=0.5),
                                 R=[psb[bg_], cst], W=thPb[ti])
                            by = ps_next()
                            mm(ps[by][:, :], src_, R=[wrb[ib]] + srcb_, W=[psb[by]])
                            if i == 0:
                                emit(DVE, lambda e, by=by, ti=ti, dd=dd: e.scalar_tensor_tensor(out=macc4[dd], in0=thP[ti], scalar=1.0, in1=ps[by][:, :],
                                                                                             op0=ALU.add, op1=ALU.mult),
                                     R=thPb[ti] + [psb[by]], W=macc4b[dd])
                            else:
                                mi = mrg_rr[1] % 2
                                mrg_rr[1] += 1
                                emit(DVE, lambda e, by=by, ti=ti, mi=mi: e.scalar_tensor_tensor(out=mtP[mi], in0=thP[ti], scalar=1.0, in1=ps[by][:, :],
                                                                                             op0=ALU.add, op1=ALU.mult),
                                     R=thPb[ti] + [psb[by]], W=mtPb[mi])
                                if i == 1:
                                    emit(DVE, lambda e, mi=mi, dd=dd: e.tensor_tensor(macc4[dd], macc4[dd], mtP[mi], ALU.add), R=mtPb[mi], W=macc4b[dd])
                                else:
                                    emit(DVE, lambda e, mi=mi, dd=dd, d=d: e.tensor_tensor(mrg[:, d, :], macc4[dd], mtP[mi], ALU.add),
                                         R=mtPb[mi] + macc4b[dd], W=mrgb)
                for eh in range(2):
                    wo = wload(w_out[l, :, eh * 512:eh * 512 + 512].rearrange("(k p) c -> p k c", p=128), 8, 512)
                    for ee in range(4):
                        e_ = eh * 4 + ee
                        b = ps_next()
                        mm(ps[b][:, :], [(wr[wo][:, d, ee * 128:(ee + 1) * 128], mrg[:, d, :]) for d in range(8)],
                           R=[wrb[wo]] + mrgb, W=[psb[b]])
                        emit(ACT, lambda e, e_=e_, b=b: e.activation(out=z3[:, e_, :], in_=ps[b][:, :], func=AF.Copy),
                             R=[psb[b]], W=[ztb])
                        emit(ACT, lambda e, e_=e_, b=b: e.activation(out=xn[:, e_, :], in_=ps[b][:, :], func=AF.Square),
                             R=[psb[b]], W=[xnb])
                postnorm_residual(PP_GPOST + l * 8, 4.0 * EPS, sq_after=True)

                prenorm(PP_GPREF + l * 8, have_sq=True)
                for tt in range(6):
                    w_ = 512 if tt < 5 else 256
                    wa = wload(w_up[l, :, tt * 512:tt * 512 + w_].rearrange("(k p) c -> p k c", p=128), 8, w_)
                    wb = wload(w_up[l, :, DFF + tt * 512:DFF + tt * 512 + w_].rearrange("(k p) c -> p k c", p=128), 8, w_)
                    for jj in range(w_ // 128):
                        j = tt * 4 + jj
                        cs = slice(jj * 128, (jj + 1) * 128)
                        st_ = []
                        for (wt, jh) in ((wa, j), (wb, NJ + j)):
                            hi = ffn_rr[0] % 3
                            ffn_rr[0] += 1
                            hX, hXb, cX, cXb, hXh = hP[hi], hPb[hi], cP[hi], cPb[hi], hPh[hi]
                            b = ps_next()
                            mm(ps[b][:, :], [(wr[wt][:, kc, cs], xn[:, kc, :]) for kc in range(8)], R=[wrb[wt], xnb], W=[psb[b]])
                            emit(ACT, lambda e, hX=hX, b=b: e.activation(out=hX[:, 2:2 + CH], in_=ps[b][:, :], func=AF.Copy),
                                 R=[psb[b]], W=hXb)
                            k0 = PP_CFW + (l * 2 * NJ + jh) * 3
                            emit(ACT, lambda e, cX=cX, k0=k0, b=b: e.activation(out=cX, in_=ps[b][:, :], func=AF.Copy, scale=pp[:, k0 + 2:k0 + 3]),
                                 R=[psb[b], cst], W=cXb)
                            st_.append((hX, hXb, cX, cXb, hXh, k0, jh))
                        gi = ffn_rr[1] % 2
                        ffn_rr[1] += 1
                        for idx, (hX, hXb, cX, cXb, hXh, k0, jh) in enumerate(st_):
                            if c == 0:
                                emit(DVE, lambda e, hX=hX: e.memset(hX[:, 0:2], 0.0), W=[hXh])
                            else:
                                emit(DVE, lambda e, hX=hX, jh=jh: e.tensor_copy(hX[:, 0:2], fha[:, jh, :]), R=[fhab[jh]], W=[hXh])
                            emit(DVE, lambda e, hX=hX, jh=jh: e.tensor_copy(fha[:, jh, :], hX[:, CH:CH + 2]), R=hXb, W=[fhab[jh]])
                            emit(DVE, lambda e, hX=hX, cX=cX, k0=k0: e.scalar_tensor_tensor(out=cX, in0=hX[:, 1:1 + CH], scalar=pp[:, k0 + 1:k0 + 2],
                                                                                          in1=cX, op0=ALU.mult, op1=ALU.add),
                                 R=hXb + [cst, hXh], W=cXb)
                            emit(DVE, lambda e, hX=hX, cX=cX, k0=k0: e.scalar_tensor_tensor(out=cX, in0=hX[:, 0:CH], scalar=pp[:, k0:k0 + 1],
                                                                                          in1=cX, op0=ALU.mult, op1=ALU.add),
                                 R=hXb + [cst, hXh], W=cXb)
                            if idx == 0:
                                emit(ACT, lambda e, cX=cX, gi=gi: e.activation(out=gP[gi], in_=cX, func=AF.Gelu_apprx_tanh), R=cXb, W=gPb[gi])
                        emit(DVE, lambda e, j=j, gi=gi, cB=st_[1][2]: e.tensor_tensor(gff[:, j, :], gP[gi], cB, ALU.mult),
                             R=gPb[gi] + st_[1][3], W=gffb(j))
                nl, ncn = (l, c + 1) if c + 1 < nch else (l + 1, 0)
                if nl < depth:
                    nt0 = ncn * CH
                    if nl == 0:
                        emit(XLD, lambda e, nt0=nt0: e.dma_start(
                            out=xpre4, in_=x_d[nt0:nt0 + CH, :].rearrange("(tb p) d -> p tb d", p=128)), W=xpreb)
                    else:
                        emit(XLD, lambda e, nt0=nt0: e.dma_start(
                            out=xpre8, in_=xT_d[:, :, nt0:nt0 + CH].rearrange("k p t -> p k t")), R=[xTb[ncn]], W=xpreb)
                    prefetched.add((nl, ncn))
                for eh in range(2):
                    bks = [ps_next() for _ in range(4)]
                    fns = []
                    Rl = []
                    for kg in range(3):
                        nk = 8 if kg < 2 else 6
                        wd = wload(w_dn[l, kg * 1024:kg * 1024 + nk * 128, eh * 512:eh * 512 + 512].rearrange("(k p) c -> p k c", p=128), nk, 512)
                        fns = []
                        for k in range(nk):
                            j = kg * 8 + k
                            for ee in range(4):
                                fns.append(lambda e, wd=wd, k=k, ee=ee, j=j, bk=bks[ee]: e.matmul(
                                    ps[bk][:, :], wr[wd][:, k, ee * 128:(ee + 1) * 128], gff[:, j, :],
                                    start=(j == 0), stop=(j == NJ - 1)))
                        emit(PE, fns, R=[wrb[wd]] + slot[0:11], W=[psb[bk] for bk in bks])
                    for ee in range(4):
                        e_ = eh * 4 + ee
                        emit(ACT, lambda e, e_=e_, bk=bks[ee]: e.activation(out=z3[:, e_, :], in_=ps[bk][:, :], func=AF.Copy),
                             R=[psb[bks[ee]]], W=[ztb])
                        emit(ACT, lambda e, e_=e_, bk=bks[ee]: e.activation(out=xn[:, e_, :], in_=ps[bk][:, :], func=AF.Square),
                             R=[psb[bks[ee]]], W=[xnb])
                postnorm_residual(PP_GPOSTF + l * 8, EPS)

                if l == depth - 1:
                    for tb in range(4):
                        for half in range(2):
                            b = ps_next()
                            emit(PE, [(lambda e, kq=kq, tb=tb, half=half, b=b: e.transpose(
                                ps[b][:, kq * 128:(kq + 1) * 128], xs[:, half * 4 + kq, tb * 128:(tb + 1) * 128], ident)) for kq in range(4)],
                                 R=[xsb, cst], W=[psb[b]])
                            emit(ACT if half else DVE,
                                 (lambda e, tb=tb, half=half, b=b: e.activation(out=xtok[:, tb, half * 512:(half + 1) * 512], in_=ps[b][:, :], func=AF.Copy))
                                 if half else
                                 (lambda e, tb=tb, half=half, b=b: e.tensor_copy(xtok[:, tb, half * 512:(half + 1) * 512], ps[b][:, :])),
                                 R=[psb[b]], W=[ztb])
                    emit(XST, lambda e, t0=t0: e.dma_start(out=y_d[t0:t0 + CH, :].rearrange("(tb p) d -> p tb d", p=128), in_=xtok),
                         R=[ztb], W=[yb])
                else:
                    emit(XST, lambda e, t0=t0: e.dma_start(out=xT_d[:, :, t0:t0 + CH].rearrange("k p t -> p k t"), in_=xs[:, :, :]),
                         R=[xsb], W=[xTb[c]])

        fin_waits = [(XST.sems[ep], (XST.cnt if ep == XST.ep else None)) for ep in range(len(XST.sems))]
        fin_waits = [(s_, n) for s_, n in fin_waits if n is not None]
        P.ops["sp"].append((fin_waits, [], None, 0))

        with nc.Block() as block:
            def run(stream):
                def f(e):
                    for waits, fns, sem, inc in P.ops[stream]:
                        for s_, n in waits:
                            e.wait_ge(s_, n)
                        ins = None
                        for fn in fns:
                            ins = fn(e)
                        if ins is not None:
                            ins.then_inc(sem, inc)
                return f
            block.tensor(run("pe"))
            block.scalar(run("act"))
            block.vector(run("dve"))
            block.gpsimd(run("pool"))
            block.sync(run("sp"))
    return nc


def _host_layout(pre_mix_g, post_mix_g, pre_ffn_g, post_ffn_g, b_forget, b_gate, conv_mix_w, sgu_ln_g, sgu_ln_b,
                 sgu_w, sgu_b, conv_ffn_w):
    f = np.float32
    pp = np.zeros((128, PP_N), f)
    for off, g in ((PP_GPRE, pre_mix_g), (PP_GPOST, post_mix_g), (PP_GPREF, pre_ffn_g), (PP_GPOSTF, post_ffn_g)):
        pp[:, off:off + 32] = np.asarray(g, f).reshape(L, 8, 128).transpose(2, 0, 1).reshape(128, 32)
    pp[:, PP_BG:PP_BG + 96] = np.asarray(b_gate, f).reshape(L, 3, 8, 128).transpose(3, 0, 1, 2).reshape(128, 96)
    pp[:, PP_CMW:PP_CMW + 24] = np.asarray(conv_mix_w, f).reshape(L, 3, 2, 128).transpose(3, 0, 2, 1).reshape(128, 24)
    pp[:, PP_LNG:PP_LNG + 8] = np.asarray(sgu_ln_g, f).reshape(L, 2, 128).transpose(2, 0, 1).reshape(128, 8)
    pp[:, PP_LNB:PP_LNB + 8] = np.asarray(sgu_ln_b, f).reshape(L, 2, 128).transpose(2, 0, 1).reshape(128, 8)
    pp[:, PP_CFW:PP_CFW + 528] = np.asarray(conv_ffn_w, f).reshape(L, 3, 2 * NJ, 128).transpose(3, 0, 2, 1).reshape(128, 528)
    pp[0:8, PP_BF:PP_BF + 4] = np.asarray(b_forget, f).T
    cc = np.zeros((128, CC_N), f)
    cc[:, CC_ID:CC_ID + 128] = np.eye(128, dtype=f)
    s_, t_ = np.meshgrid(np.arange(128), np.arange(128), indexing="ij")
    cc[:, CC_TRI:CC_TRI + 128] = (t_ >= s_).astype(f)
    cc[:, CC_NEG:CC_NEG + 128] = np.where(t_ < s_, -30000.0, 0.0).astype(f)
    sel = np.zeros((8, 1024), f)
    for h in range(8):
        sel[h, h * 128:(h + 1) * 128] = 1.0
    swT = np.ascontiguousarray(np.asarray(sgu_w, f).transpose(3, 0, 1, 2).reshape(128, L * 4 * 128))
    sb_ = np.asarray(sgu_b, f).reshape(L, 2, 2, 1, 128)
    sb_ = np.broadcast_to(sb_, (L, 2, 2, 64, 128)).transpose(2, 3, 0, 1, 4).reshape(128, L * 2 * 128)
    return pp, cc, swT, np.ascontiguousarray(sb_), sel


_NC_CACHE = {}


def kernel(x, pre_mix_g, post_mix_g, pre_ffn_g, post_ffn_g, w_in, b_forget, b_gate, conv_mix_w, sgu_ln_g, sgu_ln_b,
           sgu_w, sgu_b, w_branch_att, w_branch_conv, w_branch_sgu, w_out, w_ffn_up, conv_ffn_w, w_ffn_down,
           _depth=L, _nch=NCH):
    f = np.float32
    x = np.asarray(x, f)
    B = x.shape[0]
    pp, cc, swT, sgb, sel = _host_layout(pre_mix_g, post_mix_g, pre_ffn_g, post_ffn_g, b_forget, b_gate, conv_mix_w,
                                    sgu_ln_g, sgu_ln_b, sgu_w, sgu_b, conv_ffn_w)
    key = (_depth, _nch)
    if key not in _NC_CACHE:
        _NC_CACHE[key] = build(_depth, _nch)
    nc = _NC_CACHE[key]
    shared = {
        "w_in": np.ascontiguousarray(w_in, f), "w_att": np.ascontiguousarray(w_branch_att, f),
        "w_cv": np.ascontiguousarray(w_branch_conv, f), "w_sg": np.ascontiguousarray(w_branch_sgu, f),
        "w_out": np.ascontiguousarray(w_out, f), "w_up": np.ascontiguousarray(w_ffn_up, f),
        "w_dn": np.ascontiguousarray(w_ffn_down, f), "pp": pp, "cc": cc, "swT": swT, "sgb": sgb, "sel": sel,
    }
    in_maps = [dict(shared, x=np.ascontiguousarray(x[b])) for b in range(B)]
    res = run_bass_kernel_spmd(nc, in_maps, core_ids=list(range(B)))
    return np.stack([np.asarray(r["y"], f) for r in res.results], axis=0)
```
